# Optimizing a Trainium2 kernel written in Bass

```python
import math
import jax, jax.numpy as jnp
from jax import lax
import numpy as np

D_MODEL = 1024
BATCH = 32
SEQ = 256
DEPTH = 4
DEC_BATCH = 8
DEC_SEQ = 2048
PAST_LEN = 512

GRID_W = 64
ATT_HEADS = 4
QK_HEAD_DIM = 64
V_HEAD_DIM = 2 * QK_HEAD_DIM
ATT_WIDTH = ATT_HEADS * V_HEAD_DIM
QK_COLS = ATT_HEADS * 2 * QK_HEAD_DIM
ROPE_BASE = 10000.0
Q_BLOCK = 128
CONV_WIDTH = D_MODEL // 4
CONV_KERNEL = 31
POOL_WIDTH = D_MODEL // 4
POOL_WINDOWS = (2, 4, 8, 16)
POOL_GROUPS = 4
POOL_GROUP_DIM = POOL_WIDTH // POOL_GROUPS
N_BRANCH = 3
IN_COLS = 2 * QK_COLS + ATT_WIDTH + 2 * CONV_WIDTH + POOL_WIDTH + N_BRANCH * D_MODEL
D_FF = ((8 * D_MODEL // 3 + 127) // 128) * 128
N_MOD = 9
ALPHA = (2 * DEPTH) ** 0.25
BETA = (8 * DEPTH) ** -0.25
LN_EPS = 1e-5

kernel_name = "hybrid_diff_conv_pool_prefix_dit_step"


def layer_norm(x, g, b):
    xf = x.astype(jnp.float32)
    mu = jnp.mean(xf, axis=-1, keepdims=True)
    var = jnp.mean(jnp.square(xf - mu), axis=-1, keepdims=True)
    y = (xf - mu) * lax.rsqrt(var + LN_EPS)
    return (y * g + b).astype(x.dtype)


def rms_norm(x, g):
    xf = x.astype(jnp.float32)
    y = xf * lax.rsqrt(jnp.mean(jnp.square(xf), axis=-1, keepdims=True) + LN_EPS)
    return (y * g).astype(x.dtype)


def swiglu(h, w_in, w_down):
    gu = h @ w_in
    g, u = jnp.split(gu, 2, axis=-1)
    return (jax.nn.silu(g) * u) @ w_down


def _rotate(seg, pos):
    nf = seg.shape[-1] // 2
    inv = ROPE_BASE ** (-jnp.arange(nf, dtype=jnp.float32) / nf)
    ang = pos.astype(jnp.float32)[:, None] * inv[None, :]
    c = jnp.cos(ang)[:, None, None, :]
    s = jnp.sin(ang)[:, None, None, :]
    x1, x2 = seg[..., :nf], seg[..., nf:]
    return jnp.concatenate([x1 * c - x2 * s, x2 * c + x1 * s], axis=-1).astype(seg.dtype)


def axial_rope(x):
    L = x.shape[1]
    rows = L // GRID_W
    row = jnp.repeat(jnp.arange(rows), GRID_W)
    col = jnp.tile(jnp.arange(GRID_W), rows)
    half = x.shape[-1] // 2
    return jnp.concatenate([_rotate(x[..., :half], row), _rotate(x[..., half:], col)], axis=-1)


def diff_attention(q, k, v, lam):
    B, Lq = q.shape[0], q.shape[1]
    qb_len = min(Q_BLOCK, Lq)
    nb = Lq // qb_len
    qb = q.reshape(B, nb, qb_len, ATT_HEADS, 2, QK_HEAD_DIM).swapaxes(0, 1)
    scale = QK_HEAD_DIM ** -0.5

    def block(qi):
        s = jnp.einsum('bqhmd,bkhmd->bhmqk', qi, k).astype(jnp.float32) * scale
        p = jax.nn.softmax(s, axis=-1)
        w = p[:, :, 0] - lam * p[:, :, 1]
        return jnp.einsum('bhqk,bkhe->bqhe', w.astype(v.dtype), v)

    o = lax.map(block, qb)
    return o.swapaxes(0, 1).reshape(B, Lq, ATT_HEADS, V_HEAD_DIM)


def depthwise_conv(u, w, b):
    C = u.shape[-1]
    y = lax.conv_general_dilated(u, w[:, None, :].astype(u.dtype), window_strides=(1,),
                                 padding=[(CONV_KERNEL // 2, CONV_KERNEL // 2)],
                                 dimension_numbers=('NWC', 'WIO', 'NWC'),
                                 feature_group_count=C)
    return y + b


def multiscale_pool(u, w_grp, scale):
    B, L, _ = u.shape
    ug = u.reshape(B, L, POOL_GROUPS, POOL_GROUP_DIM).astype(jnp.float32)
    t = jnp.arange(L)
    outs = []
    for g, win in enumerate(POOL_WINDOWS):
        ui = ug[:, :, g]
        cs = jnp.concatenate([jnp.zeros_like(ui[:, :1]), jnp.cumsum(ui, axis=1)], axis=1)
        lo = jnp.clip(t - win // 2, 0, L)
        hi = jnp.clip(t + win // 2, 0, L)
        mean = (cs[:, hi] - cs[:, lo]) / (hi - lo).astype(jnp.float32)[None, :, None]
        outs.append(jnp.einsum('blc,cd->bld', (mean - ui).astype(u.dtype), w_grp[g]))
    return jnp.concatenate(outs, axis=-1) * scale


def token_mixer(h, l, p, ctx_k, ctx_v):
    B, L, _ = h.shape
    z = h @ p["w_in"][l] + p["b_in"][l]
    pts = (QK_COLS, 2 * QK_COLS, 2 * QK_COLS + ATT_WIDTH,
           2 * QK_COLS + ATT_WIDTH + 2 * CONV_WIDTH,
           2 * QK_COLS + ATT_WIDTH + 2 * CONV_WIDTH + POOL_WIDTH)
    q, k, v, cg, pu, gt = jnp.split(z, pts, axis=-1)
    q = q.reshape(B, L, ATT_HEADS, 2, QK_HEAD_DIM)
    k = k.reshape(B, L, ATT_HEADS, 2, QK_HEAD_DIM)
    v = v.reshape(B, L, ATT_HEADS, V_HEAD_DIM)
    if ctx_k is None:
        kk, vv = k, v
    else:
        q = axial_rope(q)
        kk = jnp.concatenate([ctx_k, axial_rope(k)], axis=1)
        vv = jnp.concatenate([ctx_v, v], axis=1)
    lam_init = 0.8 - 0.6 * math.exp(-0.3 * l)
    lq = p["lambda_qk"][l].astype(jnp.float32)
    lam = jnp.exp(jnp.sum(lq[0] * lq[1])) - jnp.exp(jnp.sum(lq[2] * lq[3])) + lam_init
    o = diff_attention(q, kk, vv, lam)
    o = rms_norm(o, p["subln_g"][l]) * (1.0 - lam_init)
    att_out = o.reshape(B, L, ATT_WIDTH) @ p["w_att_o"][l]
    a, g = jnp.split(cg, 2, axis=-1)
    u = a * jax.nn.sigmoid(g)
    u = depthwise_conv(u, p["conv_dw_w"][l], p["conv_dw_b"][l])
    u = jax.nn.silu(layer_norm(u, p["conv_ln_g"][l], p["conv_ln_b"][l]))
    conv_out = u @ p["w_conv_o"][l]
    pool_out = multiscale_pool(pu, p["w_pool_g"][l], p["pool_scale"][l]) @ p["w_pool_o"][l]
    ga, gc, gp = jnp.split(jax.nn.sigmoid(gt), N_BRANCH, axis=-1)
    merged = ga * att_out + gc * conv_out + gp * pool_out
    return merged @ p["w_out"][l] + p["b_out"][l], k, v


def trunk_layer(x, cond, l, p, ctx_k, ctx_v):
    mod = jax.nn.silu(cond) @ p["w_mod"][l] + p["b_mod"][l]
    mod = mod.reshape(cond.shape[0], N_MOD, D_MODEL)[:, :, None, :]

    def modulate(i):
        return x * (1.0 + mod[:, 3 * i + 1]) + mod[:, 3 * i]

    h = modulate(0)
    x = layer_norm(ALPHA * x + 0.5 * mod[:, 2] * swiglu(h, p["w_ffn_in"][l, 0], p["w_ffn_out"][l, 0]),
                   p["ln_g"][l, 0], p["ln_b"][l, 0])
    h = modulate(1)
    mix, k, v = token_mixer(h, l, p, ctx_k, ctx_v)
    x = layer_norm(ALPHA * x + mod[:, 5] * mix, p["ln_g"][l, 1], p["ln_b"][l, 1])
    h = modulate(2)
    x = layer_norm(ALPHA * x + 0.5 * mod[:, 8] * swiglu(h, p["w_ffn_in"][l, 1], p["w_ffn_out"][l, 1]),
                   p["ln_g"][l, 2], p["ln_b"][l, 2])
    return x, k, v


def setup_inputs(seed: int = 0) -> dict:
    key = jax.random.key(seed)
    ks = iter(jax.random.split(key, 40))

    def nrm(shape, scale):
        return jax.random.normal(next(ks), shape, jnp.float32) * scale

    d = {}
    d["x_prompt"] = nrm((BATCH, SEQ, D_MODEL), 1.0)
    d["x_sample"] = nrm((DEC_BATCH, DEC_SEQ, D_MODEL), 1.0)
    d["cache_k"] = nrm((DEC_BATCH, DEPTH, PAST_LEN, ATT_HEADS, 2, QK_HEAD_DIM), 1.0)
    d["cache_v"] = nrm((DEC_BATCH, DEPTH, PAST_LEN, ATT_HEADS, V_HEAD_DIM), 1.0)
    d["c"] = nrm((DEC_BATCH, D_MODEL), 1.0)
    d["c_ctx"] = nrm((D_MODEL,), 1.0)
    d["w_mod"] = nrm((DEPTH, D_MODEL, N_MOD * D_MODEL), 0.5 * D_MODEL ** -0.5)
    d["b_mod"] = nrm((DEPTH, N_MOD * D_MODEL), 0.02)
    d["w_ffn_in"] = nrm((DEPTH, 2, D_MODEL, 2 * D_FF), D_MODEL ** -0.5)
    d["w_ffn_out"] = nrm((DEPTH, 2, D_FF, D_MODEL), BETA * D_FF ** -0.5)
    d["ln_g"] = 1.0 + nrm((DEPTH, 3, D_MODEL), 0.02)
    d["ln_b"] = nrm((DEPTH, 3, D_MODEL), 0.02)
    d["w_in"] = nrm((DEPTH, D_MODEL, IN_COLS), D_MODEL ** -0.5)
    d["b_in"] = nrm((DEPTH, IN_COLS), 0.02)
    d["lambda_qk"] = nrm((DEPTH, 4, QK_HEAD_DIM), 0.1)
    d["subln_g"] = 1.0 + nrm((DEPTH, ATT_HEADS, V_HEAD_DIM), 0.02)
    d["w_att_o"] = nrm((DEPTH, ATT_WIDTH, D_MODEL), ATT_WIDTH ** -0.5)
    d["conv_dw_w"] = nrm((DEPTH, CONV_KERNEL, CONV_WIDTH), CONV_KERNEL ** -0.5)
    d["conv_dw_b"] = nrm((DEPTH, CONV_WIDTH), 0.02)
    d["conv_ln_g"] = 1.0 + nrm((DEPTH, CONV_WIDTH), 0.02)
    d["conv_ln_b"] = nrm((DEPTH, CONV_WIDTH), 0.02)
    d["w_conv_o"] = nrm((DEPTH, CONV_WIDTH, D_MODEL), CONV_WIDTH ** -0.5)
    d["w_pool_g"] = nrm((DEPTH, POOL_GROUPS, POOL_GROUP_DIM, POOL_GROUP_DIM), POOL_GROUP_DIM ** -0.5)
    d["pool_scale"] = 1.0 + nrm((DEPTH, POOL_WIDTH), 0.02)
    d["w_pool_o"] = nrm((DEPTH, POOL_WIDTH, D_MODEL), POOL_WIDTH ** -0.5)
    d["w_out"] = nrm((DEPTH, D_MODEL, D_MODEL), BETA * D_MODEL ** -0.5)
    d["b_out"] = nrm((DEPTH, D_MODEL), 0.02)
    return d


def reference(x_prompt, x_sample, cache_k, cache_v, c, c_ctx, w_mod, b_mod, w_ffn_in, w_ffn_out,
              ln_g, ln_b, w_in, b_in, lambda_qk, subln_g, w_att_o, conv_dw_w, conv_dw_b,
              conv_ln_g, conv_ln_b, w_conv_o, w_pool_g, pool_scale, w_pool_o, w_out, b_out):
    p = dict(w_mod=w_mod, b_mod=b_mod, w_ffn_in=w_ffn_in, w_ffn_out=w_ffn_out, ln_g=ln_g, ln_b=ln_b,
             w_in=w_in, b_in=b_in, lambda_qk=lambda_qk, subln_g=subln_g, w_att_o=w_att_o,
             conv_dw_w=conv_dw_w, conv_dw_b=conv_dw_b, conv_ln_g=conv_ln_g, conv_ln_b=conv_ln_b,
             w_conv_o=w_conv_o, w_pool_g=w_pool_g, pool_scale=pool_scale, w_pool_o=w_pool_o,
             w_out=w_out, b_out=b_out)
    xp = x_prompt
    cond_ctx = c_ctx[None, :]
    ks_new, vs_new = [], []
    for l in range(DEPTH):
        xp, k_l, v_l = trunk_layer(xp, cond_ctx, l, p, None, None)
        ks_new.append(k_l)
        vs_new.append(v_l)
    new_cache_k = jnp.stack(ks_new, axis=1)
    new_cache_v = jnp.stack(vs_new, axis=1)
    xs = x_sample
    for l in range(DEPTH):
        xs, _, _ = trunk_layer(xs, c, l, p, cache_k[:, l], cache_v[:, l])
    return (xp, xs, new_cache_k, new_cache_v)
```

```python
import math
import contextlib
import numpy as np
import concourse.bass as bass
import concourse.mybir as mybir
from concourse.bass_utils import run_bass_kernel_spmd

F32 = mybir.dt.float32
BF16 = mybir.dt.bfloat16
AF = mybir.ActivationFunctionType
ALU = mybir.AluOpType

D = 1024
DEPTH = 4
NCORE = 8
SEQ = 256
PB = 4
TP = PB * SEQ
TS = 2048
PAST = 512
H = 4
DFF = 2816
NJ = DFF // 128
INC = 5376
ALPHA = (2 * DEPTH) ** 0.25
LN_EPS = 1e-5
EPS_RES = LN_EPS / (ALPHA * ALPHA)
WINS = (2, 4, 8, 16)
KCONV = 31
HB = 512
import os
LN_ENG = os.environ.get("LN_ENG", "pool")

_PL = {}
_off = 0
def _padd(name, w):
    global _off
    _PL[name] = (_off, w)
    _off += w
_padd("cond", 16)
_padd("b_mod", DEPTH * 72)
_padd("ln_g", DEPTH * 3 * 8)
_padd("ln_b", DEPTH * 3 * 8)
_padd("b_in", DEPTH * 42)
_padd("b_out", DEPTH * 8)
_padd("subln", DEPTH * 4)
_padd("convw", DEPTH * 2 * KCONV)
_padd("convb", DEPTH * 2)
_padd("convlg", DEPTH * 2)
_padd("convlb", DEPTH * 2)
_padd("pscale", DEPTH * 2)
_padd("invw", 2)
_padd("corrL", 16)
_padd("corrR", 16)
_padd("crow", 32)
_padd("ccol", 64)
_padd("srow", 32)
_padd("scol", 64)
NPAR = _off


class Buf:
    __slots__ = ("name", "last_w", "readers", "ld_cnt", "st_cnt")

    def __init__(self, name):
        self.name = name
        self.last_w = None
        self.readers = []
        self.ld_cnt = 0
        self.st_cnt = 0


class Op:
    __slots__ = ("eng", "emit", "deps", "needs_inc", "seq", "dma", "sem", "ndma")

    def __init__(self, eng, emit):
        self.eng = eng
        self.emit = emit
        self.deps = []
        self.needs_inc = False
        self.seq = None
        self.dma = False
        self.sem = None
        self.ndma = 0


ENGS = ("pe", "act", "dve", "pool", "sp")
EPOCH = 30000


class Prog:
    def __init__(self, nc):
        self.nc = nc
        self.ops = {e: [] for e in ENGS}
        self.out_sems = {}
        self.phase = ""
        self.pe_phases = []

    def add(self, eng, emit, reads=(), writes=(), dma=None, ndma=1):
        op = Op(eng, emit)
        deps = []
        seen = set()

        def adddep(o):
            if o is not None and id(o) not in seen and o is not op:
                seen.add(id(o))
                deps.append(o)

        for b in reads:
            adddep(b.last_w)
        for b in writes:
            adddep(b.last_w)
            for r in b.readers:
                adddep(r)
        for b in reads:
            b.readers.append(op)
        for b in writes:
            b.last_w = op
            b.readers = []
        for d in deps:
            if d.dma:
                kind, b = d.sem
                op.deps.append(("d", d.sem, b.ld_cnt if kind == "ld" else b.st_cnt))
            elif d.eng == "pe" and eng == "pe":
                continue
            else:
                d.needs_inc = True
                op.deps.append(("c", d))
        if dma is not None:
            kind, b = dma
            op.dma = True
            op.ndma = ndma
            op.sem = (kind, b)
            if kind == "ld":
                b.ld_cnt += 16 * ndma
            else:
                b.st_cnt += 16 * ndma
                self.out_sems[id(b)] = b
        self.ops[eng].append(op)
        if eng == "pe":
            self.pe_phases.append(self.phase)
        return op

    def emit_all(self):
        nc = self.nc
        for e in ENGS:
            n = 0
            for op in self.ops[e]:
                if not op.dma and op.needs_inc:
                    n += 1
                    op.seq = n
        with contextlib.ExitStack() as st:
            esem = {}

            def engsem(e, ep):
                if (e, ep) not in esem:
                    esem[(e, ep)] = st.enter_context(nc.semaphore(f"s_{e}_{ep}"))
                return esem[(e, ep)]

            bufsems = {}

            def bsem(key):
                kind, b = key
                k = (kind, id(b))
                if k not in bufsems:
                    bufsems[k] = st.enter_context(nc.semaphore(f"{kind}_{b.name}"))
                return bufsems[k]

            for e in ENGS:
                for op in self.ops[e]:
                    if op.dma:
                        bsem(op.sem)
                    elif op.needs_inc:
                        engsem(e, (op.seq - 1) // EPOCH)
            block = st.enter_context(nc.Block())

            def run(eng_name, eng):
                waited = {}
                for op in self.ops[eng_name]:
                    for d in op.deps:
                        if d[0] == "d":
                            s = bsem(d[1])
                            key = ("d", d[1][0], id(d[1][1]))
                            v = d[2]
                        else:
                            o = d[1]
                            ep = (o.seq - 1) // EPOCH
                            s = engsem(o.eng, ep)
                            key = (o.eng, ep)
                            v = o.seq - ep * EPOCH
                        if waited.get(key, 0) >= v:
                            continue
                        waited[key] = v
                        eng.wait_ge(s, v)
                    ins = op.emit(eng)
                    if op.dma:
                        if not isinstance(ins, (list, tuple)):
                            ins = [ins]
                        assert len(ins) == op.ndma
                        for i_ in ins:
                            i_.then_inc(bsem(op.sem), 16)
                    elif op.needs_inc:
                        if isinstance(ins, (list, tuple)):
                            ins = ins[-1]
                        ins.then_inc(engsem(eng_name, (op.seq - 1) // EPOCH), 1)
                if eng_name == "sp":
                    for b in self.out_sems.values():
                        eng.wait_ge(bsem(("st", b)), b.st_cnt)

            @block.tensor
            def _(e):
                run("pe", e)

            @block.scalar
            def _(e):
                run("act", e)

            @block.vector
            def _(e):
                run("dve", e)

            @block.gpsimd
            def _(e):
                run("pool", e)

            @block.sync
            def _(e):
                run("sp", e)


def build_program(n_layers=DEPTH, groups=(0, 1), debug_taps=False):
    nc = bass.Bass("TRN2", target_bir_lowering=False)
    P = Prog(nc)
    dr = {}

    def din(name, shape):
        dr[name] = nc.dram_tensor(name, list(shape), F32, kind="ExternalInput").ap()

    def dout(name, shape):
        dr[name] = nc.dram_tensor(name, list(shape), F32, kind="ExternalOutput").ap()

    din("xpT", [D, TP]); din("xsT", [D, TS]); din("ckT", [DEPTH, 512, PAST]); din("cv", [DEPTH, PAST, 512])
    din("params", [128, NPAR]); din("lamrep", [128, DEPTH * 256]); din("permR", [128, 128])
    din("wpbd", [DEPTH, 2, 128, 128]); din("bvrep", [DEPTH, 128, 512])
    din("w_mod", [DEPTH, D, 9 * D]); din("w_ffn_in", [DEPTH, 2, D, 2 * DFF]); din("w_ffn_out", [DEPTH, 2, DFF, D])
    din("w_in", [DEPTH, D, INC]); din("w_acp", [DEPTH, D, D]); din("w_out", [DEPTH, D, D])
    dout("ypT", [D, TP]); dout("ysT", [D, TS]); dout("okT", [DEPTH, 512, TP]); dout("ov", [DEPTH, TP, 512])

    class SB:
        off = 18432
        LIMIT = 229376

    def salloc(name, shape, dtype, at=None):
        nb = int(np.prod(shape[1:])) * (4 if dtype == F32 else 2)
        nb = (nb + 31) // 32 * 32
        if at is None:
            at = SB.off
            SB.off += nb
            assert SB.off <= SB.LIMIT, (name, SB.off)
        else:
            assert at + nb <= SB.LIMIT, (name, at, nb)
        return nc.alloc_sbuf_tensor_at(name, list(shape), dtype, offset=at)

    xs = salloc("xs", [128, 8, TS], F32)
    ringF = [salloc(f"rf{i}", [128, HB], F32) for i in range(6)]
    ringF_b = [Buf(f"rf{i}") for i in range(6)]
    stats = [salloc(f"st{i}", [128, HB], F32) for i in range(5)]
    stats_b = [Buf(f"st{i}") for i in range(5)]
    NSLOT = 4
    wslot = [salloc(f"ws{i}", [128, 2048], BF16) for i in range(NSLOT)]
    wslot_b = [Buf(f"ws{i}") for i in range(NSLOT)]
    par = salloc("par", [128, NPAR], F32)
    par_b = Buf("par")
    modT = salloc("modT", [128, DEPTH, 72, 2], F32)
    modT_bs = [Buf(f"modT{l_}") for l_ in range(DEPTH)]
    md = salloc("md", [128, DEPTH, 2, 3, 3, 8], F32)
    md_bs = [Buf(f"md{l_}") for l_ in range(DEPTH)]
    ones32 = salloc("ones32", [128, 128], F32)
    ones16 = salloc("ones16", [128, 128], BF16)
    permR = salloc("permR", [128, 128], F32)
    sc16 = salloc("sc16", [128, 8, 2], BF16)
    lamt = salloc("lamt", [128, DEPTH, 4], F32)
    subg = salloc("subg", [128, DEPTH, 4], F32)
    cst_b = Buf("cst")
    lam_b = Buf("lam")
    epst = salloc("epst", [128, 4], F32)
    ARENA = SB.off
    hT = salloc("hT", [128, 8, 1024], BF16, at=ARENA)
    aT = salloc("aT", [128, NJ, 1024], BF16, at=ARENA + 16384)
    FFN_END = ARENA + 16384 + NJ * 1024 * 2
    o_ = ARENA
    hTm = salloc("hTm", [128, 8, HB], BF16, at=o_); o_ += 8192
    aoT = salloc("aoT", [128, 4, HB], BF16, at=o_); o_ += 4096
    mgT = salloc("mgT", [128, 8, HB], BF16, at=o_)
    QT = salloc("QT", [128, 4, 2, HB], BF16, at=o_); o_ += 8192
    cuo = salloc("cuo", [128, 2, HB], BF16, at=o_); o_ += 2048
    pmT = salloc("pmT", [128, 2, HB], BF16, at=o_); o_ += 2048
    pgT = salloc("pgT", [128, 2, HB], BF16, at=o_); o_ += 2048
    KT = salloc("KT", [128, 4, PAST + TS], BF16, at=o_); o_ += 4 * (PAST + TS) * 2
    VT = salloc("VT", [128, 20, 512], BF16, at=o_); o_ += 20 * 512 * 2
    assert o_ >= FFN_END, (o_, FFN_END)
    NPT = 4
    PT = [salloc(f"PT{i}", [128, HB], BF16, at=o_ + i * 1024) for i in range(NPT)]; o_ += NPT * 1024
    identb = salloc("identb", [128, 128], BF16, at=o_); o_ += 256
    bv = salloc("bv", [128, 512], F32, at=o_); o_ += 2048
    cosh = salloc("cosh", [128, HB], F32, at=o_); o_ += 2048
    sinh = salloc("sinh", [128, HB], F32, at=o_); o_ += 2048
    wpb = salloc("wpb", [128, 2, 128], BF16, at=o_); o_ += 512
    CUW = TS + 30
    PBW = TS + 16
    cu = salloc("cu", [128, 2, CUW], BF16, at=o_); o_ += (2 * CUW * 2 + 31) // 32 * 32
    pbuf = salloc("pbuf", [128, 2, PBW], BF16, at=o_); o_ += (2 * PBW * 2 + 31) // 32 * 32
    assert o_ <= SB.LIMIT, o_
    _LAYOUT_INFO.update(arena=ARENA, mix_end=o_, limit=SB.LIMIT, ffn_end=FFN_END)
    PT_b = [Buf(f"PT{i}") for i in range(NPT)]

    psum = [nc.alloc_psum_tensor(f"ps{i}", [128, HB], F32) for i in range(8)]
    psum_b = [Buf(f"ps{i}") for i in range(8)]

    class RR:
        def __init__(self, idx):
            self.idx = list(idx); self.i = 0

        def next(self):
            k = self.idx[self.i % len(self.idx)]; self.i += 1
            return k

    ring_all = RR(range(8))
    rf_rr = RR(range(6))
    ws_rr = RR(range(NSLOT))

    def ps_next(ring=None):
        k = (ring or ring_all).next()
        return psum[k], psum_b[k]

    def rf_next():
        k = rf_rr.next()
        return ringF[k], ringF_b[k]

    hT_b = [[Buf(f"hT{h}_{c}") for c in range(8)] for h in range(2)]
    aT_b = [[Buf(f"aT{j}_{h}") for h in range(2)] for j in range(NJ)]
    hTm_b = [Buf(f"hTm{c}") for c in range(8)]; QT_b = [Buf(f"QT{h}") for h in range(4)]; aoT_b = [Buf(f"ao{h}") for h in range(4)]
    mgT_b = [Buf(f"mg{c}") for c in range(8)]
    cuo_b = Buf("cuo"); pmT_b = Buf("pmT"); pgT_b = Buf("pgT"); cu_b = Buf("cu"); pbuf_b = Buf("pbuf")
    KT_b = Buf("KT"); VT_b = Buf("VT"); bv_b = Buf("bv"); rope_b = Buf("rope"); wpb_b = Buf("wpb")
    xs_b = [[Buf(f"xs{c}_{t}") for t in range(TS // HB)] for c in range(8)]
    ffn_arena = [b for row in hT_b for b in row] + [b for row in aT_b for b in row]
    mix_arena = hTm_b + [cuo_b, pmT_b, pgT_b, KT_b, VT_b, bv_b, rope_b, wpb_b] + QT_b + aoT_b + mgT_b + PT_b

    def phase_sync(to_mixer):
        leaving = ffn_arena if to_mixer else mix_arena
        entering = mix_arena if to_mixer else ffn_arena
        P.add("dve", lambda e: e.memset(lamt[:, 0, 3:4], 0.0), reads=leaving + [lam_b], writes=entering)

    def apm(base, dims):
        return bass.AP(tensor=base.tensor, offset=base.offset, ap=[list(base.ap[0])] + [list(d_) for d_ in dims])

    def pcol(name, idx=0, w=1):
        o, _ = _PL[name]
        return par[:, o + idx:o + idx + w]

    def act(out, in_, func, reads, writes, bias=None, scale=None):
        kw = {}
        if bias is not None:
            kw["bias"] = bias
        if scale is not None:
            kw["scale"] = scale
        P.add("act", lambda e: e.activation(out=out, in_=in_, func=func, **kw), reads=reads, writes=writes)

    def tt(eng, out, in0, in1, op, reads, writes):
        P.add(eng, lambda e: e.tensor_tensor(out=out, in0=in0, in1=in1, op=op), reads=reads, writes=writes)

    def ts(eng, out, in0, s1, s2, op0, op1, reads, writes):
        if s2 is None:
            P.add(eng, lambda e: e.tensor_scalar(out=out, in0=in0, scalar1=s1, scalar2=None, op0=op0), reads=reads, writes=writes)
        else:
            P.add(eng, lambda e: e.tensor_scalar(out=out, in0=in0, scalar1=s1, scalar2=s2, op0=op0, op1=op1), reads=reads, writes=writes)

    def stt(eng, out, in0, scalar, in1, op0, op1, reads, writes):
        P.add(eng, lambda e: e.scalar_tensor_tensor(out=out, in0=in0, scalar=scalar, in1=in1, op0=op0, op1=op1), reads=reads, writes=writes)

    def mm(out, lhsT, rhs, start, stop, reads, writes):
        P.add("pe", lambda e: e.matmul(out, lhsT=lhsT, rhs=rhs, start=start, stop=stop), reads=reads, writes=writes)

    def wload(parts):
        k = ws_rr.next()
        slot, sb = wslot[k], wslot_b[k]
        views = []
        dmas = []
        for (co, kk, nn, src) in parts:
            v = slot[:, co:co + kk * nn].rearrange("p (k n) -> p k n", k=kk)
            views.append(v)
            dmas.append((v, src))

        def emit(e, dmas=dmas):
            return [e.dma_start(out=v, in_=s) for (v, s) in dmas]

        P.add("pool", emit, writes=[sb], dma=("ld", sb), ndma=len(dmas))
        return views, sb

    def wrows(w2d, r0, nk, c0, n):
        return w2d[r0:r0 + nk * 128, c0:c0 + n].rearrange("(k p) n -> p k n", p=128)

    P.add("sp", lambda e: e.dma_start(out=par[:], in_=dr["params"]), writes=[par_b], dma=("ld", par_b))
    P.add("sp", lambda e: e.dma_start(out=permR[:], in_=dr["permR"]), writes=[cst_b], dma=("ld", cst_b))
    P.add("dve", lambda e: e.memset(ones32[:], 1.0), writes=[cst_b])
    P.add("dve", lambda e: e.memset(ones16[:], 1.0), writes=[cst_b])
    P.add("pool", lambda e: e.memset(identb[:], 1.0), writes=[cst_b])
    P.add("pool", lambda e: e.affine_select(out=identb[:], in_=identb[:], pattern=[[-1, 128]], compare_op=ALU.is_equal, fill=0.0,
                                            base=0, channel_multiplier=1), reads=[cst_b], writes=[cst_b])
    P.add("dve", lambda e: e.memset(epst[:, 0:1], float(EPS_RES)), writes=[cst_b])
    P.add("dve", lambda e: e.memset(epst[:, 1:2], float(LN_EPS)), writes=[cst_b])
    lr, lrb = rf_next()
    lr2, lr2b = rf_next()
    P.add("sp", lambda e: e.dma_start(out=lr[:], in_=dr["lamrep"][:, 0:512]), writes=[lrb], dma=("ld", lrb))
    P.add("sp", lambda e: e.dma_start(out=lr2[:], in_=dr["lamrep"][:, 512:1024]), writes=[lr2b], dma=("ld", lr2b))
    for l in range(DEPTH):
        src, srcb = (lr, lrb) if l < 2 else (lr2, lr2b)
        o = (l % 2) * 256
        lam_init = 0.8 - 0.6 * math.exp(-0.3 * l)
        for q in range(2):
            a0 = src[:, o + q * 128:o + q * 128 + 64]
            a1 = src[:, o + q * 128 + 64:o + q * 128 + 128]
            tt("dve", a0, a0, a1, ALU.mult, [srcb], [srcb])
            P.add("dve", lambda e, a0=a0, l=l, q=q: e.reduce_sum(out=lamt[:, l, 1 + q:2 + q], in_=a0, axis=mybir.AxisListType.X),
                  reads=[srcb], writes=[lam_b])
        act(lamt[:, l, 1:3], lamt[:, l, 1:3], AF.Exp, [lam_b], [lam_b])
        stt("dve", lamt[:, l, 0:1], lamt[:, l, 2:3], -lam_init, lamt[:, l, 1:2], ALU.add, ALU.subtract, [lam_b], [lam_b])
        ts("dve", subg[:, l, :], pcol("subln", l * 4, 4), 1.0 - lam_init, None, ALU.mult, None, [par_b], [lam_b])

    P.phase = "mod"
    o_c, _ = _PL["cond"]
    condv = par[:, o_c:o_c + 16].rearrange("p (k g) -> p k g", k=8)
    act(sc16[:], condv, AF.Silu, [par_b], [cst_b])
    MOD_BANK = 7

    def mod_closures(l):
        pm_, pmb = psum[MOD_BANK], psum_b[MOD_BANK]
        fns = []

        def tile(jt):
            (wv,), wb = wload([(0, 8, 256, wrows(dr["w_mod"][l], 0, 8, jt * 256, 256))])
            for jj in range(2):
                j = jt * 2 + jj
                for kc in range(8):
                    mm(pm_[:, j * 2:j * 2 + 2], wv[:, kc, jj * 128:(jj + 1) * 128], sc16[:, kc, :], kc == 0, kc == 7,
                       [wb, cst_b], [pmb])

        def fin():
            ob, _ = _PL["b_mod"]
            tt("dve", modT[:, l], pm_[:, 0:144].rearrange("p (j g) -> p j g", g=2),
               apm(par[:, ob + l * 72: ob + l * 72 + 1], [[1, 72], [0, 2]]), ALU.add, [pmb, par_b], [modT_bs[l]])
            for g in range(2):
                for i in range(3):
                    ts("dve", md[:, l, g, i, 0, :], modT[:, l, (3 * i + 1) * 8:(3 * i + 2) * 8, g], 1.0, None, ALU.add, None,
                       [modT_bs[l]], [md_bs[l]])
                    P.add("dve", lambda e, g=g, i=i: e.tensor_copy(out=md[:, l, g, i, 1, :], in_=modT[:, l, (3 * i) * 8:(3 * i + 1) * 8, g]),
                          reads=[modT_bs[l]], writes=[md_bs[l]])
                    ts("dve", md[:, l, g, i, 2, :], modT[:, l, (3 * i + 2) * 8:(3 * i + 3) * 8, g], (0.5 if i != 1 else 1.0) / ALPHA, None,
                       ALU.mult, None, [modT_bs[l]], [md_bs[l]])

        for jt in range(36):
            fns.append(lambda jt=jt: tile(jt))
        fns.append(fin)
        return fns

    inject_mod = (0 in groups)
    for l in range(n_layers if not inject_mod else 1):
        for fn in mod_closures(l):
            fn()

    def modulate(l, g, i, tok0, ntok, dst, dst_bufs_by_half, half0):
        tbs = set(range(tok0 // HB, (tok0 + ntok) // HB))
        dq_flush(lambda tag: tag.get("tb") in tbs)
        for c in range(8):
            for hh in range(ntok // HB):
                tb = (tok0 // HB) + hh
                src = xs[:, c, tok0 + hh * HB: tok0 + (hh + 1) * HB]
                out = dst[:, c, hh * HB:(hh + 1) * HB]
                A = md[:, l, g, i, 0, c:c + 1]
                B = md[:, l, g, i, 1, c:c + 1]
                if (c + hh) % 2 == 0:
                    act(out, src, AF.Identity, [xs_b[c][tb], md_bs[l]], [dst_bufs_by_half[hh][c]], bias=B, scale=A)
                else:
                    ts("dve", out, src, A, B, ALU.mult, ALU.add, [xs_b[c][tb], md_bs[l]], [dst_bufs_by_half[hh][c]])

    DQ = []
    st_state = {"next": 0}

    def dq_push(tag, fn):
        DQ.append((tag, fn))

    def dq_pop(n=1):
        for _ in range(n):
            if not DQ:
                return
            _, fn = DQ.pop(0)
            fn()

    def dq_flush(pred=None):
        last = -1
        for i_, (tag, _) in enumerate(DQ):
            if pred is None or pred(tag):
                last = i_
        for _ in range(last + 1):
            _, fn = DQ.pop(0)
            fn()

    def ln_stats(chunks, nfeat, eps):
        k = st_state["next"]
        st_state["next"] = 1 - k
        dq_flush(lambda tag: tag.get("sset") == k)
        mean, mean_b, rstd, rstd_b, tmp, tmp_b = stats[2 * k], stats_b[2 * k], stats[2 * k + 1], stats_b[2 * k + 1], stats[4], stats_b[4]
        s1, s1b = ps_next()
        s2, s2b = ps_next()
        n = len(chunks)
        for ci, (ap_, bufs) in enumerate(chunks):
            sq, sqb = rf_next()
            sq16 = sq[:, 0:HB // 2].bitcast(BF16)
            y16 = sq[:, HB // 2:HB].bitcast(BF16)
            if ci < 5:
                act(y16, ap_, AF.Identity, bufs, [sqb])
                act(sq16, ap_, AF.Square, bufs, [sqb])
            else:
                P.add("dve", lambda e, y16=y16, ap_=ap_: e.tensor_copy(out=y16, in_=ap_), reads=bufs, writes=[sqb])
                tt("dve", sq16, ap_, ap_, ALU.mult, bufs, [sqb])
            mm(s1[:], ones16[:], y16, ci == 0, ci == n - 1, [cst_b, sqb], [s1b])
            mm(s2[:], ones16[:], sq16, ci == 0, ci == n - 1, [cst_b, sqb], [s2b])
        inv = 1.0 / nfeat
        act(mean[:], s1[:], AF.Identity, [s1b], [mean_b], scale=inv)
        act(tmp[:], s1[:], AF.Square, [s1b], [tmp_b], scale=inv)
        stt("dve", tmp[:], s2[:], inv, tmp[:], ALU.mult, ALU.subtract, [s2b, tmp_b], [tmp_b])
        act(rstd[:], tmp[:], AF.Ln, [tmp_b, cst_b], [rstd_b], bias=epsc(eps), scale=1.0)
        act(rstd[:], rstd[:], AF.Exp, [rstd_b], [rstd_b], scale=-0.5)
        return k

    def epsc(v):
        return epst[:, 0:1] if v == EPS_RES else epst[:, 1:2]

    def ln_apply_res(l, i, tb, k):
        og, _ = _PL["ln_g"]
        ob, _ = _PL["ln_b"]
        mean, mean_b, rstd, rstd_b = stats[2 * k], stats_b[2 * k], stats[2 * k + 1], stats_b[2 * k + 1]

        def one(c):
            x_ = xs[:, c, tb * HB:(tb + 1) * HB]
            xb_ = xs_b[c][tb]
            tt(LN_ENG, x_, x_, mean[:], ALU.subtract, [xb_, mean_b], [xb_])
            tt("dve", x_, x_, rstd[:], ALU.mult, [xb_, rstd_b], [xb_])
            kk = (l * 3 + i) * 8 + c
            ts("dve", x_, x_, par[:, og + kk:og + kk + 1], par[:, ob + kk:ob + kk + 1], ALU.mult, ALU.add, [xb_, par_b], [xb_])

        for c in range(8):
            dq_push({"tb": tb, "sset": k}, lambda c=c: one(c))

    def ffn(l, f, g, tok0, do_mod=True, post_in_hook=None):
        P.phase = f"ffn"
        i = 0 if f == 0 else 2
        tb0 = tok0 // HB
        if do_mod:
            modulate(l, g, i, tok0, 1024, hT, hT_b, 0)
        w_in2 = dr["w_ffn_in"][l, f]
        w_out2 = dr["w_ffn_out"][l, f]
        for j in range(NJ):
            (wg, wu), wb = wload([(0, 8, 128, wrows(w_in2, 0, 8, j * 128, 128)),
                                  (1024, 8, 128, wrows(w_in2, 0, 8, DFF + j * 128, 128))])
            for hh in range(2):
                G, Gb = ps_next()
                U, Ub = ps_next()
                for kc in range(8):
                    mm(G[:], wg[:, kc, :], hT[:, kc, hh * HB:(hh + 1) * HB], kc == 0, kc == 7, [wb, hT_b[hh][kc]], [Gb])
                for kc in range(8):
                    mm(U[:], wu[:, kc, :], hT[:, kc, hh * HB:(hh + 1) * HB], kc == 0, kc == 7, [wb, hT_b[hh][kc]], [Ub])
                sg, sgb = rf_next()
                act(sg[:], G[:], AF.Silu, [Gb], [sgb])
                tt("dve", aT[:, j, hh * HB:(hh + 1) * HB], sg[:], U[:], ALU.mult, [sgb, Ub], [aT_b[j][hh]])
                dq_pop(1)
        if post_in_hook is not None:
            post_in_hook()
        for m in range(8):
            (w0,), wb0 = wload([(0, 11, 128, wrows(w_out2, 0, 11, m * 128, 128))])
            (w1,), wb1 = wload([(0, 11, 128, wrows(w_out2, 11 * 128, 11, m * 128, 128))])
            for hh in range(2):
                O, Ob = ps_next()
                for j in range(NJ):
                    wv, wb = (w0, wb0) if j < 11 else (w1, wb1)
                    mm(O[:], wv[:, j % 11, :], aT[:, j, hh * HB:(hh + 1) * HB], j == 0, j == NJ - 1, [wb, aT_b[j][hh]], [Ob])
                x_ = xs[:, m, tok0 + hh * HB: tok0 + (hh + 1) * HB]
                stt("dve", x_, O[:], md[:, l, g, i, 2, m:m + 1], x_, ALU.mult, ALU.add, [Ob, md_bs[l], xs_b[m][tb0 + hh]], [xs_b[m][tb0 + hh]])
        P.phase = "ffn_ln"
        for hh in range(2):
            tb = tb0 + hh
            k_ = ln_stats([(xs[:, c, tb * HB:(tb + 1) * HB], [xs_b[c][tb]]) for c in range(8)], D, EPS_RES)
            ln_apply_res(l, i, tb, k_)

    ring_sc = RR([4, 5, 6, 7])
    ring_acc = RR([0, 1, 2, 3])

    def mixer_A(l, g, tb, nseq, S, kt_col0, v_chunk0, cu_cols, pb_cols, do_mod=True):
        tok0 = tb * HB
        P.phase = f"mixA_g{g}"
        if do_mod:
            modulate(l, g, 1, tok0, HB, hTm, [hTm_b], 0)
        win = dr["w_in"][l]
        ob_in, _ = _PL["b_in"]
        for hp in range(2):
            (wk,), wb = wload([(0, 8, 256, wrows(win, 0, 8, 512 + hp * 256, 256))])
            for hh2 in range(2):
                h = hp * 2 + hh2
                Z, Zb = ps_next()
                for kc in range(8):
                    mm(Z[:], wk[:, kc, hh2 * 128:(hh2 + 1) * 128], hTm[:, kc, :], kc == 0, kc == 7, [wb, hTm_b[kc]], [Zb])
                k32, k32b = rf_next()
                bcol = par[:, ob_in + l * 42 + 4 + h: ob_in + l * 42 + 5 + h]
                act(k32[:], Z[:], AF.Identity, [Zb, par_b], [k32b], bias=bcol, scale=1.0)
                kdst = KT[:, h, kt_col0:kt_col0 + HB]
                if g == 0:
                    P.add("sp", lambda e, l=l, h=h, tok0=tok0, k32=k32: e.dma_start(
                        out=dr["okT"][l, h * 128:(h + 1) * 128, tok0:tok0 + HB], in_=k32[:]), reads=[k32b], dma=("st", k32b))
                    P.add("dve", lambda e, kdst=kdst, k32=k32: e.tensor_copy(out=kdst, in_=k32[:]), reads=[k32b], writes=[KT_b])
                else:
                    rope(k32, k32b, kdst, KT_b)
                dq_pop(1)
        (wv0,), wvb0 = wload([(0, 8, 256, wrows(win, 0, 8, 1024, 256))])
        (wv1,), wvb1 = wload([(0, 8, 256, wrows(win, 0, 8, 1280, 256))])
        for t4 in range(4):
            Z, Zb = ps_next()
            for (wv_, wvb_, c0) in ((wv0, wvb0, 0), (wv1, wvb1, 256)):
                for kc in range(8):
                    mm(Z[:, c0:c0 + 256], hTm[:, kc, t4 * 128:(t4 + 1) * 128], wv_[:, kc, :], kc == 0, kc == 7, [wvb_, hTm_b[kc]], [Zb])
            vdst = VT[:, v_chunk0 + t4, :]
            if g == 0:
                v32, v32b = rf_next()
                tt("dve", v32[:], Z[:], bv[:], ALU.add, [Zb, bv_b], [v32b])
                P.add("sp", lambda e, l=l, t0=tok0 + t4 * 128, v32=v32: e.dma_start(out=dr["ov"][l, t0:t0 + 128, :], in_=v32[:]),
                      reads=[v32b], dma=("st", v32b))
                act(vdst, v32[:], AF.Identity, [v32b], [VT_b])
            else:
                tt("dve", vdst, Z[:], bv[:], ALU.add, [Zb, bv_b], [VT_b])
            dq_pop(1)
        (wa,), wab = wload([(0, 8, 256, wrows(win, 0, 8, 1536, 256))])
        (wg_,), wgb = wload([(0, 8, 256, wrows(win, 0, 8, 1792, 256))])
        for c in range(2):
            Za, Zab = ps_next()
            Zg, Zgb = ps_next()
            for kc in range(8):
                mm(Za[:], wa[:, kc, c * 128:(c + 1) * 128], hTm[:, kc, :], kc == 0, kc == 7, [wab, hTm_b[kc]], [Zab])
            for kc in range(8):
                mm(Zg[:], wg_[:, kc, c * 128:(c + 1) * 128], hTm[:, kc, :], kc == 0, kc == 7, [wgb, hTm_b[kc]], [Zgb])
            sg, sgb = rf_next()
            act(sg[:], Zg[:], AF.Sigmoid, [Zgb, par_b], [sgb], bias=par[:, ob_in + l * 42 + 14 + c: ob_in + l * 42 + 15 + c], scale=1.0)
            ba = par[:, ob_in + l * 42 + 12 + c: ob_in + l * 42 + 13 + c]
            for s in range(nseq):
                stt("dve", cu[:, c, cu_cols[s]:cu_cols[s] + S if nseq > 1 else cu_cols[s] + HB],
                    Za[:, s * S:(s + 1) * S] if nseq > 1 else Za[:], ba,
                    sg[:, s * S:(s + 1) * S] if nseq > 1 else sg[:], ALU.add, ALU.mult, [Zab, sgb, par_b], [cu_b])
            dq_pop(1)
        (wp,), wpb_ = wload([(0, 8, 256, wrows(win, 0, 8, 2048, 256))])
        for c in range(2):
            Zp, Zpb = ps_next()
            for kc in range(8):
                mm(Zp[:], wp[:, kc, c * 128:(c + 1) * 128], hTm[:, kc, :], kc == 0, kc == 7, [wpb_, hTm_b[kc]], [Zpb])
            bp = par[:, ob_in + l * 42 + 16 + c: ob_in + l * 42 + 17 + c]
            for s in range(nseq):
                act(pbuf[:, c, pb_cols[s]:pb_cols[s] + (S if nseq > 1 else HB)],
                    Zp[:, s * S:(s + 1) * S] if nseq > 1 else Zp[:], AF.Identity, [Zpb, par_b], [pbuf_b], bias=bp, scale=1.0)

    def rope(q32, q32b, dst, dst_b, split=None):
        Rq, Rqb = ps_next()
        mm(Rq[:], permR[:], q32[:], True, True, [cst_b, q32b], [Rqb])
        t2, t2b = rf_next()
        tt("dve", t2[:], Rq[:], sinh[:], ALU.mult, [Rqb, rope_b], [t2b])
        tt("dve", q32[:], q32[:], cosh[:], ALU.mult, [q32b, rope_b], [q32b])
        if split is None:
            tt("dve", dst, q32[:], t2[:], ALU.add, [q32b, t2b], [dst_b])
        else:
            tt("dve", split[0], q32[0:64, :], t2[0:64, :], ALU.add, [q32b, t2b], dst_b)
            tt("dve", split[1], q32[64:128, :], t2[64:128, :], ALU.add, [q32b, t2b], dst_b)

    def rope_tables(tb):
        for name_r, name_c, dstt in (("crow", "ccol", cosh), ("srow", "scol", sinh)):
            orow, _ = _PL[name_r]
            ocol, _ = _PL[name_c]
            a_r = apm(par[:, orow + tb * 8: orow + tb * 8 + 1], [[1, 8], [0, 64]])
            a_c = apm(par[:, ocol:ocol + 1], [[0, 8], [1, 64]])
            tt("dve", dstt[:].rearrange("p (r c) -> p r c", r=8), a_r, a_c, ALU.mult, [par_b], [rope_b])

    def attention(l, g, nq, qcol0, key_chunks, tail_hook=None):
        nk = len(key_chunks)
        items = [(h, m, ki) for h in range(H) for m in range(2) for ki in range(nk)]
        LA = 3
        DEFER = min(10, nk)
        acc = {}
        tm = {}
        pending = []

        def finish_group(h, m):
            O, Ob, Sm, Smb = acc[(h, m)]
            r_, rb = rf_next()
            if g == 0:
                act(r_[:, 0:nq], Sm[:, 0:nq], AF.Ln, [Smb], [rb])
                act(r_[:, 0:nq], r_[:, 0:nq], AF.Exp, [rb], [rb], scale=-1.0)
            else:
                P.add("dve", lambda e: e.reciprocal(out=r_[:, 0:nq], in_=Sm[:, 0:nq]), reads=[Smb], writes=[rb])
            t_, tb_ = rf_next()
            tt("dve", t_[:, 0:nq], O[:, 0:nq], r_[:, 0:nq], ALU.mult, [Ob, rb], [tb_])
            tm[(h, m)] = (t_, tb_)

        def finish_head_a(h):
            (t0, t0b), (t1, t1b) = tm[(h, 0)], tm[(h, 1)]
            stt("dve", t0[:, 0:nq], t1[:, 0:nq], lamt[:, l, 0:1], t0[:, 0:nq], ALU.mult, ALU.add, [t1b, t0b, lam_b], [t0b])
            if nk < 8:
                sq, sqb = rf_next()
                act(sq[:, 0:HB // 2].bitcast(BF16)[:, 0:nq], t0[:, 0:nq], AF.Square, [t0b], [sqb])
                return (t0, t0b, sq, sqb)
            return (t0, t0b)

        def finish_head_b(h, st_):
            if len(st_) == 4:
                t0, t0b, sq, sqb = st_
            else:
                t0, t0b = st_
                sq, sqb = rf_next()
                act(sq[:, 0:HB // 2].bitcast(BF16)[:, 0:nq], t0[:, 0:nq], AF.Square, [t0b], [sqb])
            Ms, Msb = ps_next(ring_sc)
            mm(Ms[:, 0:nq], ones16[:], sq[:, 0:HB // 2].bitcast(BF16)[:, 0:nq], True, True, [cst_b, sqb], [Msb])
            act(sq[:, 0:nq], Ms[:, 0:nq], AF.Ln, [Msb, cst_b], [sqb], bias=epsc(LN_EPS), scale=1.0 / 128.0)
            act(sq[:, 0:nq], sq[:, 0:nq], AF.Exp, [sqb], [sqb], scale=-0.5)
            stt("dve", aoT[:, h, qcol0:qcol0 + nq], t0[:, 0:nq], subg[:, l, h:h + 1], sq[:, 0:nq], ALU.mult, ALU.mult,
                [t0b, sqb, lam_b], [aoT_b[h]])

        n = len(items)
        for idx in range(n + LA):
            if idx < n:
                h, m, ki = items[idx]
                if ki == 0:
                    O, Ob = ps_next(ring_acc)
                    Sm, Smb = ps_next(ring_acc)
                    acc[(h, m)] = (O, Ob, Sm, Smb)
                kcol, vch = key_chunks[ki]
                p0 = m * 64
                Sc, Scb = ps_next(ring_sc)
                mm(Sc[:, 0:nq], KT[:, h, kcol:kcol + 128], QT[:, h, m, qcol0:qcol0 + nq], True, True,
                   [KT_b, QT_b[h]], [Scb])
                pi = idx % NPT
                act(PT[pi][:, 0:nq], Sc[:, 0:nq], AF.Exp, [Scb], [PT_b[pi]], scale=0.125)
                if idx % 4 == 3:
                    dq_pop(1)
            j = idx - LA
            if j >= 0:
                h, m, ki = items[j]
                kcol, vch = key_chunks[ki]
                O, Ob, Sm, Smb = acc[(h, m)]
                pi = j % NPT
                mm(O[:, 0:nq], VT[:, vch, h * 128:(h + 1) * 128], PT[pi][:, 0:nq], ki == 0, ki == nk - 1, [VT_b, PT_b[pi]], [Ob])
                mm(Sm[:, 0:nq], ones16[:], PT[pi][:, 0:nq], ki == 0, ki == nk - 1, [cst_b, PT_b[pi]], [Smb])
                if ki == nk - 1:
                    finish_group(h, m)
                    if m == 1:
                        st_ = finish_head_a(h)
                        pending.append((j + DEFER, h, st_))
                while pending and pending[0][0] <= j:
                    _, hh_, st_ = pending.pop(0)
                    finish_head_b(hh_, st_)
        if tail_hook is not None:
            tail_hook()
        for (_, hh_, st_) in pending:
            finish_head_b(hh_, st_)

    def mixer_B(l, g, tb, nseq, S, cu_cols, pb_cols, attn_specs, edge_specs, do_mod=True, post_merge_hook=None):
        tok0 = tb * HB
        win = dr["w_in"][l]
        ob_in, _ = _PL["b_in"]
        P.phase = f"mixB_q_g{g}"
        if g == 1 and do_mod:
            modulate(l, g, 1, tok0, HB, hTm, [hTm_b], 0)
            rope_tables(tb)
        def pool_block():
            P.phase = "pool"
            PBS = SEQ + 16
            for c in range(2):
                acc, accb = rf_next()
                for gi in range(2):
                    w = WINS[2 * c + gi]
                    p0 = gi * 64
                    first = True
                    for jx in range(-w // 2 + 1, w // 2):
                        if nseq > 1:
                            a_ = apm(acc[p0:p0 + 64, 0:1], [[S, nseq], [1, S]])
                            src = apm(pbuf[p0:p0 + 64, c, pb_cols[0] + jx: pb_cols[0] + jx + 1], [[PBS, nseq], [1, S]])
                            src0 = apm(pbuf[p0:p0 + 64, c, pb_cols[0] - w // 2: pb_cols[0] - w // 2 + 1], [[PBS, nseq], [1, S]])
                        else:
                            a_ = acc[p0:p0 + 64, 0:HB]
                            src = pbuf[p0:p0 + 64, c, pb_cols[0] + jx: pb_cols[0] + jx + HB]
                            src0 = pbuf[p0:p0 + 64, c, pb_cols[0] - w // 2: pb_cols[0] - w // 2 + HB]
                        if first:
                            tt("dve", a_, src0, src, ALU.add, [pbuf_b], [accb])
                        else:
                            tt("dve", a_, a_, src, ALU.add, [pbuf_b, accb], [accb])
                        first = False
                if nseq > 1:
                    for (col, nm) in ((0, "corrL"), (S - 8, "corrR")):
                        oc, _ = _PL[nm]
                        e_ = apm(acc[:, col:col + 1], [[S, nseq], [1, 8]])
                        tt("dve", e_, e_, apm(par[:, oc + c * 8: oc + c * 8 + 1], [[0, nseq], [1, 8]]), ALU.mult, [accb, par_b], [accb])
                    stt("dve", pmT[:, c, :].rearrange("p (s n) -> p s n", s=nseq), acc[:].rearrange("p (s n) -> p s n", s=nseq),
                        pcol("invw", c), apm(pbuf[:, c, pb_cols[0]: pb_cols[0] + 1], [[PBS, nseq], [1, S]]),
                        ALU.mult, ALU.subtract, [accb, par_b, pbuf_b], [pmT_b])
                else:
                    for (col, which) in edge_specs:
                        oc, _ = _PL["corrL" if which == 0 else "corrR"]
                        tt("dve", acc[:, col:col + 8], acc[:, col:col + 8], par[:, oc + c * 8: oc + c * 8 + 8], ALU.mult, [accb, par_b], [accb])
                    stt("dve", pmT[:, c, :], acc[:], pcol("invw", c), pbuf[:, c, pb_cols[0]: pb_cols[0] + HB],
                        ALU.mult, ALU.subtract, [accb, par_b, pbuf_b], [pmT_b])
                def pg_part(c=c):
                    Pg, Pgb = ps_next(ring_sc)
                    mm(Pg[:], wpb[:, c, :], pmT[:, c, :], True, True, [wpb_b, pmT_b], [Pgb])
                    act(pgT[:, c, :], Pg[:], AF.Identity, [Pgb, par_b], [pgT_b], scale=pcol("pscale", l * 2 + c), bias=0.0)
                for _ in range(3):
                    dq_push({"need": "merge"}, lambda: None)
                dq_push({"need": "merge"}, pg_part)

        def q_block():
            P.phase = f"mixB_q_g{g}"
            for hp in range(2):
                (wq,), wb = wload([(0, 8, 256, wrows(win, 0, 8, hp * 256, 256))])
                for hh2 in range(2):
                    h = hp * 2 + hh2
                    Z, Zb = ps_next()
                    for kc in range(8):
                        mm(Z[:], wq[:, kc, hh2 * 128:(hh2 + 1) * 128], hTm[:, kc, :], kc == 0, kc == 7, [wb, hTm_b[kc]], [Zb])
                    bcol = par[:, ob_in + l * 42 + h: ob_in + l * 42 + h + 1]
                    qw = [QT_b[h], mgT_b[2 * h], mgT_b[2 * h + 1]]
                    P.add("pool", lambda e, h=h: e.memset(QT[64:128, h, 0, :], 0.0), writes=qw)
                    P.add("pool", lambda e, h=h: e.memset(QT[0:64, h, 1, :], 0.0), writes=qw)
                    if g == 0:
                        act(QT[0:64, h, 0, :], Z[0:64, :], AF.Identity, [Zb, par_b], qw, bias=bcol[0:64, :], scale=1.0)
                        act(QT[64:128, h, 1, :], Z[64:128, :], AF.Identity, [Zb, par_b], qw, bias=bcol[64:128, :], scale=1.0)
                    else:
                        q32, q32b = rf_next()
                        act(q32[:], Z[:], AF.Identity, [Zb, par_b], [q32b], bias=bcol, scale=1.0)
                        rope(q32, q32b, None, qw, split=(QT[0:64, h, 0, :], QT[64:128, h, 1, :]))

        ocw, _ = _PL["convw"]

        def build_diags(eng):
            out = []
            for c in range(2):
                dgs = []
                for (k0, nkk) in ((0, 16), (16, 15)):
                    sk = ws_rr.next()
                    slot, sb = wslot[sk], wslot_b[sk]
                    dv_ = slot[:, 0:nkk * 128].rearrange("p (k n) -> p k n", k=nkk)
                    w_ap = apm(par[:, ocw + (l * 2 + c) * KCONV + k0: ocw + (l * 2 + c) * KCONV + k0 + 1], [[1, nkk], [0, 128]])
                    i_ap = apm(identb[:, 0:1], [[0, nkk], [1, 128]])
                    P.add(eng, lambda e, dv_=dv_, w_ap=w_ap, i_ap=i_ap: e.tensor_tensor(out=dv_, in0=i_ap, in1=w_ap, op=ALU.mult),
                          reads=[par_b, cst_b], writes=[sb])
                    dgs.append((dv_, sb, k0, nkk))
                out.append(dgs)
            return out

        accs = []

        def conv_mm(dg_all):
            P.phase = "conv"
            for c in range(2):
                Cp, Cpb = ps_next()
                for (dv_, sb, k0, nkk) in dg_all[c]:
                    for kk in range(nkk):
                        k = k0 + kk
                        if nseq > 1:
                            rhs = apm(cu[:, c, cu_cols[0] + k - 15: cu_cols[0] + k - 14], [[SEQ + 30, nseq], [1, S]])
                        else:
                            rhs = cu[:, c, cu_cols[0] + k - 15: cu_cols[0] + k - 15 + HB]
                        mm(Cp[:], dv_[:, kk, :], rhs, k == 0, k == KCONV - 1, [sb, cu_b], [Cpb])
                acc, accb = rf_next()
                act(acc[:], Cp[:], AF.Identity, [Cpb, par_b], [accb], bias=pcol("convb", l * 2 + c), scale=1.0)
                accs.append((acc, accb))

        if g == 0:
            dg_all = build_diags("dve")
            pool_block()
            conv_mm(dg_all)
            q_block()
        else:
            q_block()
            conv_mm(build_diags("pool"))
        P.phase = "conv"
        kc_ = ln_stats([(a_[:], [ab_]) for (a_, ab_) in accs], 256, LN_EPS)
        for c in range(2):
            acc, accb = accs[c]
            tt("dve", acc[:], acc[:], stats[2 * kc_][:], ALU.subtract, [accb, stats_b[2 * kc_]], [accb])
            tt("dve", acc[:], acc[:], stats[2 * kc_ + 1][:], ALU.mult, [accb, stats_b[2 * kc_ + 1]], [accb])
            act(acc[:], acc[:], AF.Identity, [accb, par_b], [accb], bias=pcol("convlb", l * 2 + c), scale=pcol("convlg", l * 2 + c))
            act(cuo[:, c, :], acc[:], AF.Silu, [accb], [cuo_b])
        if g == 1:
            pool_block()
        P.phase = f"attn_g{g}"
        for (nq, qcol0, key_chunks) in attn_specs:
            attention(l, g, nq, qcol0, key_chunks)
        dq_flush(lambda tag: tag.get("need") == "merge")
        P.phase = "merge"
        for fc in range(8):
            (wg2,), wgab = wload([(0, 8, 256, wrows(win, 0, 8, 2304 + fc * 256, 256))])
            wga, wgc, wgcb = wg2[:, :, 0:128], wg2[:, :, 128:256], wgab
            (wgp,), wgpb = wload([(0, 8, 128, wrows(win, 0, 8, 2304 + 2048 + fc * 128, 128))])
            (wacp,), wob = wload([(0, 8, 128, wrows(dr["w_acp"][l], 0, 8, fc * 128, 128))])
            wao, wco, wpo = wacp[:, 0:4, :], wacp[:, 4:6, :], wacp[:, 6:8, :]
            gts = []
            for gi, (wg_, wgb_) in enumerate(((wga, wgab), (wgc, wgcb), (wgp, wgpb))):
                Zg, Zgb = ps_next()
                for kc in range(8):
                    mm(Zg[:], wg_[:, kc, :], hTm[:, kc, :], kc == 0, kc == 7, [wgb_, hTm_b[kc]], [Zgb])
                sg, sgb = rf_next()
                zc = 18 + gi * 8 + fc
                act(sg[:], Zg[:], AF.Sigmoid, [Zgb, par_b], [sgb], bias=par[:, ob_in + l * 42 + zc: ob_in + l * 42 + zc + 1], scale=1.0)
                gts.append((sg, sgb))
            A_, Ab = ps_next()
            for h in range(H):
                mm(A_[:], wao[:, h, :], aoT[:, h, :], h == 0, h == H - 1, [wob, aoT_b[h]], [Ab])
            C_, Cb = ps_next()
            for c in range(2):
                mm(C_[:], wco[:, c, :], cuo[:, c, :], c == 0, c == 1, [wob, cuo_b], [Cb])
            Pp, Ppb = ps_next()
            for c in range(2):
                mm(Pp[:], wpo[:, c, :], pgT[:, c, :], c == 0, c == 1, [wob, pgT_b], [Ppb])
            (ga, gab), (gc, gcb), (gp, gpb) = gts
            tt("dve", ga[:], ga[:], A_[:], ALU.mult, [gab, Ab], [gab])
            tt("dve", gc[:], gc[:], C_[:], ALU.mult, [gcb, Cb], [gcb])
            tt("dve", gp[:], gp[:], Pp[:], ALU.mult, [gpb, Ppb], [gpb])
            tt("dve", ga[:], ga[:], gc[:], ALU.add, [gab, gcb], [gab])
            tt("dve", mgT[:, fc, :], ga[:], gp[:], ALU.add, [gab, gpb], [mgT_b[fc], QT_b[fc // 2]])
            dq_pop(1)
        if post_merge_hook is not None:
            post_merge_hook()
        P.phase = "outproj"
        obo, _ = _PL["b_out"]
        for fp in range(4):
            (wo,), wob = wload([(0, 8, 256, wrows(dr["w_out"][l], 0, 8, fp * 256, 256))])
            for f2 in range(2):
                fo = fp * 2 + f2
                O, Ob = ps_next()
                for fc in range(8):
                    mm(O[:], wo[:, fc, f2 * 128:(f2 + 1) * 128], mgT[:, fc, :], fc == 0, fc == 7, [wob, mgT_b[fc]], [Ob])
                t_, tb_ = rf_next()
                act(t_[:], O[:], AF.Identity, [Ob, par_b], [tb_], bias=par[:, obo + l * 8 + fo: obo + l * 8 + fo + 1], scale=1.0)
                x_ = xs[:, fo, tok0:tok0 + HB]
                stt("dve", x_, t_[:], md[:, l, g, 1, 2, fo:fo + 1], x_, ALU.mult, ALU.add, [tb_, md_bs[l], xs_b[fo][tb]], [xs_b[fo][tb]])
        P.phase = "mix_ln"
        k_ = ln_stats([(xs[:, c, tok0:tok0 + HB], [xs_b[c][tb]]) for c in range(8)], D, EPS_RES)
        ln_apply_res(l, 1, tb, k_)

    def mixer(l, g):
        phase_sync(True)
        P.add("sp", lambda e: e.dma_start(out=bv[:], in_=dr["bvrep"][l]), writes=[bv_b], dma=("ld", bv_b))
        P.add("pool", lambda e: e.dma_start(out=wpb[:], in_=dr["wpbd"][l].rearrange("c p n -> p c n")), writes=[wpb_b], dma=("ld", wpb_b))
        if g == 0:
            for tb in range(TP // HB):
                cu_cols = [15 + s * (SEQ + 30) for s in range(2)]
                pb_cols = [8 + s * (SEQ + 16) for s in range(2)]
                mixer_A(l, g, tb, 2, SEQ, tb * HB, tb * 4, cu_cols, pb_cols, do_mod=(tb == 0))
                specs = [(SEQ, s * SEQ, [(tb * HB + s * SEQ + kk * 128, tb * 4 + s * 2 + kk) for kk in range(2)]) for s in range(2)]
                edges = [(s * SEQ, 0) for s in range(2)] + [(s * SEQ + SEQ - 8, 1) for s in range(2)]
                hook = (lambda tb=tb: modulate(l, g, 1, (tb + 1) * HB, HB, hTm, [hTm_b], 0)) if tb + 1 < TP // HB else None
                mixer_B(l, g, tb, 2, SEQ, cu_cols, pb_cols, specs, edges, post_merge_hook=hook)
        else:
            P.add("pool", lambda e: e.dma_start(out=KT[:, :, 0:PAST], in_=dr["ckT"][l].rearrange("(h p) t -> p h t", p=128)),
                  writes=[KT_b], dma=("ld", KT_b))
            P.add("pool", lambda e: e.dma_start(out=VT[:, 0:4, :], in_=dr["cv"][l].rearrange("(k p) n -> p k n", p=128)),
                  writes=[VT_b], dma=("ld", VT_b))
            for tb in range(TS // HB):
                rope_tables(tb)
                mixer_A(l, g, tb, 1, TS, PAST + tb * HB, 4 + tb * 4, [15 + tb * HB], [8 + tb * HB])
            keys = [(kk * 128, kk) for kk in range(20)]
            for tb in range(TS // HB):
                edges = ([(0, 0)] if tb == 0 else []) + ([(HB - 8, 1)] if tb == TS // HB - 1 else [])
                def hook_s(tb=tb):
                    modulate(l, g, 1, (tb + 1) * HB, HB, hTm, [hTm_b], 0)
                    rope_tables(tb + 1)
                mixer_B(l, g, tb, 1, TS, [15 + tb * HB], [8 + tb * HB], [(HB, 0, keys)], edges, do_mod=(tb == 0),
                        post_merge_hook=hook_s if tb + 1 < TS // HB else None)
        phase_sync(False)

    def run_ffn(l, f, g, T):
        nmt = T // 1024
        for mt in range(nmt):
            hook = None
            if mt + 1 < nmt:
                hook = (lambda mt=mt: modulate(l, g, 0 if f == 0 else 2, (mt + 1) * 1024, 1024, hT, hT_b, 0))
            ffn(l, f, g, mt * 1024, do_mod=(mt == 0), post_in_hook=hook)

    for g in groups:
        T = TP if g == 0 else TS
        xin = dr["xpT"] if g == 0 else dr["xsT"]
        yout = dr["ypT"] if g == 0 else dr["ysT"]
        for c in range(8):
            P.add("sp", lambda e, c=c, xin=xin, T=T: e.dma_start(out=xs[:, c, 0:T], in_=xin[c * 128:(c + 1) * 128, :]),
                  writes=[xs_b[c][tb] for tb in range(T // HB)], dma=("ld", xs_b[c][0]))
        P.add("dve", lambda e: e.memset(cu[:], 0.0), writes=[cu_b])
        P.add("dve", lambda e: e.memset(pbuf[:], 0.0), writes=[pbuf_b])
        phase_sync(False)
        for l in range(n_layers):
            if g == 0 and inject_mod and l + 1 < n_layers:
                ring_all.idx = [k_ for k_ in range(8) if k_ != MOD_BANK]
                for fn in mod_closures(l + 1):
                    dq_push({"mod": l + 1}, fn)
            run_ffn(l, 0, g, T)
            if g == 0 and inject_mod and l + 1 < n_layers:
                dq_flush(lambda tag: tag.get("mod") == l + 1)
                ring_all.idx = list(range(8))
            mixer(l, g)
            run_ffn(l, 1, g, T)
        dq_flush()
        for c in range(8):
            P.add("sp", lambda e, c=c, yout=yout, T=T: e.dma_start(out=yout[c * 128:(c + 1) * 128, :], in_=xs[:, c, 0:T]),
                  reads=[xs_b[c][tb] for tb in range(T // HB)], dma=("st", xs_b[c][0]))
    _LAYOUT_INFO["pe_phases"] = P.pe_phases
    P.emit_all()
    return nc


def _fm(v):
    v = np.asarray(v, np.float32)
    lead = v.shape[:-1]
    n = v.shape[-1] // 128
    return np.moveaxis(v.reshape(lead + (n, 128)), -1, 0).reshape(128, -1)


def _const_tables():
    par = np.zeros((128, NPAR), np.float32)
    invw = np.zeros((128, 2), np.float32)
    corrL = np.ones((128, 2, 8), np.float32)
    corrR = np.ones((128, 2, 8), np.float32)
    for c in range(2):
        for gi in range(2):
            w = WINS[2 * c + gi]
            sl = slice(gi * 64, gi * 64 + 64)
            invw[sl, c] = 1.0 / w
            for e in range(8):
                cntL = min(e + w // 2, 10 ** 9) - max(e - w // 2, 0)
                corrL[sl, c, e] = w / cntL
                hi_off = min(-8 + e + w // 2, 0)
                lo_off = -8 + e - w // 2
                corrR[sl, c, e] = w / (hi_off - lo_off)
    inv = 10000.0 ** (-np.arange(16, dtype=np.float32) / 16.0)
    crow = np.ones((128, 32), np.float32); srow = np.ones((128, 32), np.float32)
    ccol = np.ones((128, 64), np.float32); scol = np.ones((128, 64), np.float32)
    for p in range(128):
        dd = p % 64
        seg = dd // 32
        i = (dd % 32) % 16
        if seg == 0:
            ang = np.arange(32, dtype=np.float32) * inv[i]
            crow[p] = np.cos(ang); srow[p] = np.sin(ang)
        else:
            ang = np.arange(64, dtype=np.float32) * inv[i]
            ccol[p] = np.cos(ang); scol[p] = np.sin(ang)
    R = np.zeros((128, 128), np.float32)
    for po in range(128):
        if po % 32 < 16:
            R[po + 16, po] = -1.0
        else:
            R[po - 16, po] = 1.0

    def put(name, arr):
        o, w = _PL[name]
        arr = np.asarray(arr, np.float32).reshape(128, -1)
        assert arr.shape[1] == w, (name, arr.shape, w)
        par[:, o:o + w] = arr

    put("invw", invw); put("corrL", corrL); put("corrR", corrR)
    put("crow", crow); put("ccol", ccol); put("srow", srow); put("scol", scol)
    return par, R, put


_NC_CACHE = {}
_LAYOUT_INFO = {}


def _perm_w_in(w_in):
    w = w_in.copy()
    ga = w_in[..., 2304:2304 + 1024].reshape(w_in.shape[:-1] + (8, 1, 128))
    gc = w_in[..., 2304 + 1024:2304 + 2048].reshape(w_in.shape[:-1] + (8, 1, 128))
    w[..., 2304:2304 + 2048] = np.concatenate([ga, gc], axis=-2).reshape(w_in.shape[:-1] + (2048,))
    return w


def prepare_inputs(x_prompt, x_sample, cache_k, cache_v, c, c_ctx, w_mod, b_mod, w_ffn_in, w_ffn_out, ln_g, ln_b, w_in, b_in,
                   lambda_qk, subln_g, w_att_o, conv_dw_w, conv_dw_b, conv_ln_g, conv_ln_b, w_conv_o, w_pool_g, pool_scale,
                   w_pool_o, w_out, b_out, cores=range(NCORE)):
    f32 = lambda a: np.ascontiguousarray(np.asarray(a, dtype=np.float32))
    par0, R, put = _const_tables()
    put("b_mod", _fm(f32(b_mod)))
    put("ln_g", _fm(f32(ln_g))); put("ln_b", _fm(f32(ln_b)))
    put("b_in", _fm(f32(b_in))); put("b_out", _fm(f32(b_out)))
    put("subln", np.moveaxis(f32(subln_g), -1, 0))
    put("convw", np.transpose(f32(conv_dw_w).reshape(DEPTH, KCONV, 2, 128), (3, 0, 2, 1)))
    put("convb", _fm(f32(conv_dw_b))); put("convlg", _fm(f32(conv_ln_g))); put("convlb", _fm(f32(conv_ln_b)))
    put("pscale", _fm(f32(pool_scale)))
    lamrep = np.ascontiguousarray(np.broadcast_to(f32(lambda_qk).reshape(1, -1), (128, DEPTH * 256)))
    wg = f32(w_pool_g)
    wpbd = np.zeros((DEPTH, 2, 128, 128), np.float32)
    for cc in range(2):
        for gi in range(2):
            wpbd[:, cc, gi * 64:(gi + 1) * 64, gi * 64:(gi + 1) * 64] = wg[:, 2 * cc + gi]
    bvrep = np.ascontiguousarray(np.broadcast_to(f32(b_in)[:, None, 1024:1536], (DEPTH, 128, 512)))
    shared = dict(lamrep=lamrep, permR=R, wpbd=wpbd, bvrep=bvrep, w_mod=f32(w_mod), w_ffn_in=f32(w_ffn_in),
                  w_ffn_out=f32(w_ffn_out), w_in=_perm_w_in(f32(w_in)),
                  w_acp=np.ascontiguousarray(np.concatenate([f32(w_att_o), f32(w_conv_o), f32(w_pool_o)], axis=1)),
                  w_out=f32(w_out))
    xp = f32(x_prompt); xsm = f32(x_sample); ck = f32(cache_k); cvv = f32(cache_v); cc_ = f32(c); cctx = f32(c_ctx)
    in_maps = []
    for core in cores:
        par = par0.copy()
        cond = np.stack([cctx, cc_[core]], axis=-1)
        o, w = _PL["cond"]
        par[:, o:o + w] = np.transpose(cond.reshape(8, 128, 2), (1, 0, 2)).reshape(128, 16)
        m = dict(shared)
        m["params"] = par
        m["xpT"] = np.ascontiguousarray(xp[core * PB:(core + 1) * PB].reshape(TP, D).T)
        m["xsT"] = np.ascontiguousarray(xsm[core].T)
        m["ckT"] = np.ascontiguousarray(np.transpose(ck[core].reshape(DEPTH, PAST, 512), (0, 2, 1)))
        m["cv"] = np.ascontiguousarray(cvv[core].reshape(DEPTH, PAST, 512))
        in_maps.append(m)
    return in_maps


def assemble_outputs(results, cores=range(NCORE)):
    cores = list(cores)
    n = len(cores)
    y_prompt = np.empty((n * PB, SEQ, D), np.float32)
    y_sample = np.empty((n, TS, D), np.float32)
    nk = np.empty((n * PB, DEPTH, SEQ, H, 2, 64), np.float32)
    nv = np.empty((n * PB, DEPTH, SEQ, H, 128), np.float32)
    for i in range(n):
        r = results[i]
        y_prompt[i * PB:(i + 1) * PB] = r["ypT"].T.reshape(PB, SEQ, D)
        y_sample[i] = r["ysT"].T
        okT = r["okT"]
        nk[i * PB:(i + 1) * PB] = np.transpose(okT, (2, 0, 1)).reshape(PB, SEQ, DEPTH, H, 2, 64).transpose(0, 2, 1, 3, 4, 5)
        ov = r["ov"]
        nv[i * PB:(i + 1) * PB] = np.transpose(ov.reshape(DEPTH, PB, SEQ, H, 128), (1, 0, 2, 3, 4))
    return (y_prompt, y_sample, nk, nv)


def kernel(**inputs):
    if "nc" not in _NC_CACHE:
        _NC_CACHE["nc"] = build_program()
    nc = _NC_CACHE["nc"]
    in_maps = prepare_inputs(**inputs)
    res = run_bass_kernel_spmd(nc, in_maps, core_ids=list(range(NCORE)))
    return assemble_outputs(res.results)
```

```python
import math
import contextlib
import numpy as np
import concourse.bass as bass
import concourse.mybir as mybir
from concourse.bass_utils import run_bass_kernel_spmd

F32 = mybir.dt.float32
BF16 = mybir.dt.bfloat16
AF = mybir.ActivationFunctionType
ALU = mybir.AluOpType

D = 1024
DEPTH = 4
NCORE = 8
SEQ = 256
PB = 4
TP = PB * SEQ
TS = 2048
PAST = 512
H = 4
DFF = 2816
NJ = DFF // 128
INC = 5376
ALPHA = (2 * DEPTH) ** 0.25
LN_EPS = 1e-5
EPS_RES = LN_EPS / (ALPHA * ALPHA)
WINS = (2, 4, 8, 16)
KCONV = 31
HB = 512
import os
LN_ENG = os.environ.get("LN_ENG", "pool")

_PL = {}
_off = 0
def _padd(name, w):
    global _off
    _PL[name] = (_off, w)
    _off += w
_padd("cond", 16)
_padd("b_mod", DEPTH * 72)
_padd("ln_g", DEPTH * 3 * 8)
_padd("ln_b", DEPTH * 3 * 8)
_padd("b_in", DEPTH * 42)
_padd("b_out", DEPTH * 8)
_padd("subln", DEPTH * 4)
_padd("convw", DEPTH * 2 * KCONV)
_padd("convb", DEPTH * 2)
_padd("convlg", DEPTH * 2)
_padd("convlb", DEPTH * 2)
_padd("pscale", DEPTH * 2)
_padd("invw", 2)
_padd("corrL", 16)
_padd("corrR", 16)
_padd("crow", 32)
_padd("ccol", 64)
_padd("srow", 32)
_padd("scol", 64)
NPAR = _off


class Buf:
    __slots__ = ("name", "last_w", "readers", "ld_cnt", "st_cnt")

    def __init__(self, name):
        self.name = name
        self.last_w = None
        self.readers = []
        self.ld_cnt = 0
        self.st_cnt = 0


class Op:
    __slots__ = ("eng", "emit", "deps", "needs_inc", "seq", "dma", "sem", "ndma")

    def __init__(self, eng, emit):
        self.eng = eng
        self.emit = emit
        self.deps = []
        self.needs_inc = False
        self.seq = None
        self.dma = False
        self.sem = None
        self.ndma = 0


ENGS = ("pe", "act", "dve", "pool", "sp")
EPOCH = 30000


class Prog:
    def __init__(self, nc):
        self.nc = nc
        self.ops = {e: [] for e in ENGS}
        self.out_sems = {}
        self.phase = ""
        self.pe_phases = []

    def add(self, eng, emit, reads=(), writes=(), dma=None, ndma=1):
        op = Op(eng, emit)
        deps = []
        seen = set()

        def adddep(o):
            if o is not None and id(o) not in seen and o is not op:
                seen.add(id(o))
                deps.append(o)

        for b in reads:
            adddep(b.last_w)
        for b in writes:
            adddep(b.last_w)
            for r in b.readers:
                adddep(r)
        for b in reads:
            b.readers.append(op)
        for b in writes:
            b.last_w = op
            b.readers = []
        for d in deps:
            if d.dma:
                kind, b = d.sem
                op.deps.append(("d", d.sem, b.ld_cnt if kind == "ld" else b.st_cnt))
            elif d.eng == "pe" and eng == "pe":
                continue
            else:
                d.needs_inc = True
                op.deps.append(("c", d))
        if dma is not None:
            kind, b = dma
            op.dma = True
            op.ndma = ndma
            op.sem = (kind, b)
            if kind == "ld":
                b.ld_cnt += 16 * ndma
            else:
                b.st_cnt += 16 * ndma
                self.out_sems[id(b)] = b
        self.ops[eng].append(op)
        if eng == "pe":
            self.pe_phases.append(self.phase)
        return op

    def emit_all(self):
        nc = self.nc
        for e in ENGS:
            n = 0
            for op in self.ops[e]:
                if not op.dma and op.needs_inc:
                    n += 1
                    op.seq = n
        with contextlib.ExitStack() as st:
            esem = {}

            def engsem(e, ep):
                if (e, ep) not in esem:
                    esem[(e, ep)] = st.enter_context(nc.semaphore(f"s_{e}_{ep}"))
                return esem[(e, ep)]

            bufsems = {}

            def bsem(key):
                kind, b = key
                k = (kind, id(b))
                if k not in bufsems:
                    bufsems[k] = st.enter_context(nc.semaphore(f"{kind}_{b.name}"))
                return bufsems[k]

            for e in ENGS:
                for op in self.ops[e]:
                    if op.dma:
                        bsem(op.sem)
                    elif op.needs_inc:
                        engsem(e, (op.seq - 1) // EPOCH)
            block = st.enter_context(nc.Block())

            def run(eng_name, eng):
                waited = {}
                for op in self.ops[eng_name]:
                    for d in op.deps:
                        if d[0] == "d":
                            s = bsem(d[1])
                            key = ("d", d[1][0], id(d[1][1]))
                            v = d[2]
                        else:
                            o = d[1]
                            ep = (o.seq - 1) // EPOCH
                            s = engsem(o.eng, ep)
                            key = (o.eng, ep)
                            v = o.seq - ep * EPOCH
                        if waited.get(key, 0) >= v:
                            continue
                        waited[key] = v
                        eng.wait_ge(s, v)
                    ins = op.emit(eng)
                    if op.dma:
                        if not isinstance(ins, (list, tuple)):
                            ins = [ins]
                        assert len(ins) == op.ndma
                        for i_ in ins:
                            i_.then_inc(bsem(op.sem), 16)
                    elif op.needs_inc:
                        if isinstance(ins, (list, tuple)):
                            ins = ins[-1]
                        ins.then_inc(engsem(eng_name, (op.seq - 1) // EPOCH), 1)
                if eng_name == "sp":
                    for b in self.out_sems.values():
                        eng.wait_ge(bsem(("st", b)), b.st_cnt)

            @block.tensor
            def _(e):
                run("pe", e)

            @block.scalar
            def _(e):
                run("act", e)

            @block.vector
            def _(e):
                run("dve", e)

            @block.gpsimd
            def _(e):
                run("pool", e)

            @block.sync
            def _(e):
                run("sp", e)


def build_program(n_layers=DEPTH, groups=(0, 1), debug_taps=False):
    nc = bass.Bass("TRN2", target_bir_lowering=False)
    P = Prog(nc)
    dr = {}

    def din(name, shape):
        dr[name] = nc.dram_tensor(name, list(shape), F32, kind="ExternalInput").ap()

    def dout(name, shape):
        dr[name] = nc.dram_tensor(name, list(shape), F32, kind="ExternalOutput").ap()

    din("xpT", [D, TP]); din("xsT", [D, TS]); din("ckT", [DEPTH, 512, PAST]); din("cv", [DEPTH, PAST, 512])
    din("params", [128, NPAR]); din("lamrep", [128, DEPTH * 256]); din("permR", [128, 128])
    din("wpbd", [DEPTH, 2, 128, 128]); din("bvrep", [DEPTH, 128, 512])
    din("w_mod", [DEPTH, D, 9 * D]); din("w_ffn_in", [DEPTH, 2, D, 2 * DFF]); din("w_ffn_out", [DEPTH, 2, DFF, D])
    din("w_in", [DEPTH, D, INC]); din("w_acp", [DEPTH, D, D]); din("w_out", [DEPTH, D, D])
    dout("ypT", [D, TP]); dout("ysT", [D, TS]); dout("okT", [DEPTH, 512, TP]); dout("ov", [DEPTH, TP, 512])

    class SB:
        off = 18432
        LIMIT = 229376

    def salloc(name, shape, dtype, at=None):
        nb = int(np.prod(shape[1:])) * (4 if dtype == F32 else 2)
        nb = (nb + 31) // 32 * 32
        if at is None:
            at = SB.off
            SB.off += nb
            assert SB.off <= SB.LIMIT, (name, SB.off)
        else:
            assert at + nb <= SB.LIMIT, (name, at, nb)
        return nc.alloc_sbuf_tensor_at(name, list(shape), dtype, offset=at)

    xs = salloc("xs", [128, 8, TS], F32)
    ringF = [salloc(f"rf{i}", [128, HB], F32) for i in range(6)]
    ringF_b = [Buf(f"rf{i}") for i in range(6)]
    stats = [salloc(f"st{i}", [128, HB], F32) for i in range(5)]
    stats_b = [Buf(f"st{i}") for i in range(5)]
    NSLOT = 4
    wslot = [salloc(f"ws{i}", [128, 2048], BF16) for i in range(NSLOT)]
    wslot_b = [Buf(f"ws{i}") for i in range(NSLOT)]
    par = salloc("par", [128, NPAR], F32)
    par_b = Buf("par")
    modT = salloc("modT", [128, DEPTH, 72, 2], F32)
    modT_bs = [Buf(f"modT{l_}") for l_ in range(DEPTH)]
    md = salloc("md", [128, DEPTH, 2, 3, 3, 8], F32)
    md_bs = [Buf(f"md{l_}") for l_ in range(DEPTH)]
    ones32 = salloc("ones32", [128, 128], F32)
    ones16 = salloc("ones16", [128, 128], BF16)
    permR = salloc("permR", [128, 128], F32)
    sc16 = salloc("sc16", [128, 8, 2], BF16)
    lamt = salloc("lamt", [128, DEPTH, 4], F32)
    subg = salloc("subg", [128, DEPTH, 4], F32)
    cst_b = Buf("cst")
    lam_b = Buf("lam")
    epst = salloc("epst", [128, 4], F32)
    ARENA = SB.off
    hT = salloc("hT", [128, 8, 1024], BF16, at=ARENA)
    aT = salloc("aT", [128, NJ, 1024], BF16, at=ARENA + 16384)
    FFN_END = ARENA + 16384 + NJ * 1024 * 2
    o_ = ARENA
    hTm = salloc("hTm", [128, 8, HB], BF16, at=o_); o_ += 8192
    aoT = salloc("aoT", [128, 4, HB], BF16, at=o_); o_ += 4096
    mgT = salloc("mgT", [128, 8, HB], BF16, at=o_)
    QT = salloc("QT", [128, 4, 2, HB], BF16, at=o_); o_ += 8192
    cuo = salloc("cuo", [128, 2, HB], BF16, at=o_); o_ += 2048
    pmT = salloc("pmT", [128, 2, HB], BF16, at=o_); o_ += 2048
    pgT = salloc("pgT", [128, 2, HB], BF16, at=o_); o_ += 2048
    KT = salloc("KT", [128, 4, PAST + TS], BF16, at=o_); o_ += 4 * (PAST + TS) * 2
    VT = salloc("VT", [128, 20, 512], BF16, at=o_); o_ += 20 * 512 * 2
    assert o_ >= FFN_END, (o_, FFN_END)
    NPT = 4
    PT = [salloc(f"PT{i}", [128, HB], BF16, at=o_ + i * 1024) for i in range(NPT)]; o_ += NPT * 1024
    identb = salloc("identb", [128, 128], BF16, at=o_); o_ += 256
    bv = salloc("bv", [128, 512], F32, at=o_); o_ += 2048
    cosh = salloc("cosh", [128, HB], F32, at=o_); o_ += 2048
    sinh = salloc("sinh", [128, HB], F32, at=o_); o_ += 2048
    wpb = salloc("wpb", [128, 2, 128], BF16, at=o_); o_ += 512
    CUW = TS + 30
    PBW = TS + 16
    cu = salloc("cu", [128, 2, CUW], BF16, at=o_); o_ += (2 * CUW * 2 + 31) // 32 * 32
    pbuf = salloc("pbuf", [128, 2, PBW], BF16, at=o_); o_ += (2 * PBW * 2 + 31) // 32 * 32
    assert o_ <= SB.LIMIT, o_
    _LAYOUT_INFO.update(arena=ARENA, mix_end=o_, limit=SB.LIMIT, ffn_end=FFN_END)
    PT_b = [Buf(f"PT{i}") for i in range(NPT)]

    psum = [nc.alloc_psum_tensor(f"ps{i}", [128, HB], F32) for i in range(8)]
    psum_b = [Buf(f"ps{i}") for i in range(8)]

    class RR:
        def __init__(self, idx):
            self.idx = list(idx); self.i = 0

        def next(self):
            k = self.idx[self.i % len(self.idx)]; self.i += 1
            return k

    ring_all = RR(range(8))
    rf_rr = RR(range(6))
    ws_rr = RR(range(NSLOT))

    def ps_next(ring=None):
        k = (ring or ring_all).next()
        return psum[k], psum_b[k]

    def rf_next():
        k = rf_rr.next()
        return ringF[k], ringF_b[k]

    hT_b = [[Buf(f"hT{h}_{c}") for c in range(8)] for h in range(2)]
    aT_b = [[Buf(f"aT{j}_{h}") for h in range(2)] for j in range(NJ)]
    hTm_b = [Buf(f"hTm{c}") for c in range(8)]; QT_b = [Buf(f"QT{h}") for h in range(4)]; aoT_b = [Buf(f"ao{h}") for h in range(4)]
    mgT_b = [Buf(f"mg{c}") for c in range(8)]
    cuo_b = Buf("cuo"); pmT_b = Buf("pmT"); pgT_b = Buf("pgT"); cu_b = Buf("cu"); pbuf_b = Buf("pbuf")
    KT_b = Buf("KT"); VT_b = Buf("VT"); bv_b = Buf("bv"); rope_b = Buf("rope"); wpb_b = Buf("wpb")
    xs_b = [[Buf(f"xs{c}_{t}") for t in range(TS // HB)] for c in range(8)]
    ffn_arena = [b for row in hT_b for b in row] + [b for row in aT_b for b in row]
    mix_arena = hTm_b + [cuo_b, pmT_b, pgT_b, KT_b, VT_b, bv_b, rope_b, wpb_b] + QT_b + aoT_b + mgT_b + PT_b

    def phase_sync(to_mixer):
        leaving = ffn_arena if to_mixer else mix_arena
        entering = mix_arena if to_mixer else ffn_arena
        P.add("dve", lambda e: e.memset(lamt[:, 0, 3:4], 0.0), reads=leaving + [lam_b], writes=entering)

    def apm(base, dims):
        return bass.AP(tensor=base.tensor, offset=base.offset, ap=[list(base.ap[0])] + [list(d_) for d_ in dims])

    def pcol(name, idx=0, w=1):
        o, _ = _PL[name]
        return par[:, o + idx:o + idx + w]

    def act(out, in_, func, reads, writes, bias=None, scale=None):
        kw = {}
        if bias is not None:
            kw["bias"] = bias
        if scale is not None:
            kw["scale"] = scale
        P.add("act", lambda e: e.activation(out=out, in_=in_, func=func, **kw), reads=reads, writes=writes)

    def tt(eng, out, in0, in1, op, reads, writes):
        P.add(eng, lambda e: e.tensor_tensor(out=out, in0=in0, in1=in1, op=op), reads=reads, writes=writes)

    def ts(eng, out, in0, s1, s2, op0, op1, reads, writes):
        if s2 is None:
            P.add(eng, lambda e: e.tensor_scalar(out=out, in0=in0, scalar1=s1, scalar2=None, op0=op0), reads=reads, writes=writes)
        else:
            P.add(eng, lambda e: e.tensor_scalar(out=out, in0=in0, scalar1=s1, scalar2=s2, op0=op0, op1=op1), reads=reads, writes=writes)

    def stt(eng, out, in0, scalar, in1, op0, op1, reads, writes):
        P.add(eng, lambda e: e.scalar_tensor_tensor(out=out, in0=in0, scalar=scalar, in1=in1, op0=op0, op1=op1), reads=reads, writes=writes)

    def mm(out, lhsT, rhs, start, stop, reads, writes):
        P.add("pe", lambda e: e.matmul(out, lhsT=lhsT, rhs=rhs, start=start, stop=stop), reads=reads, writes=writes)

    def wload(parts):
        k = ws_rr.next()
        slot, sb = wslot[k], wslot_b[k]
        views = []
        dmas = []
        for (co, kk, nn, src) in parts:
            v = slot[:, co:co + kk * nn].rearrange("p (k n) -> p k n", k=kk)
            views.append(v)
            dmas.append((v, src))

        def emit(e, dmas=dmas):
            return [e.dma_start(out=v, in_=s) for (v, s) in dmas]

        P.add("pool", emit, writes=[sb], dma=("ld", sb), ndma=len(dmas))
        return views, sb

    def wrows(w2d, r0, nk, c0, n):
        return w2d[r0:r0 + nk * 128, c0:c0 + n].rearrange("(k p) n -> p k n", p=128)

    P.add("sp", lambda e: e.dma_start(out=par[:], in_=dr["params"]), writes=[par_b], dma=("ld", par_b))
    P.add("sp", lambda e: e.dma_start(out=permR[:], in_=dr["permR"]), writes=[cst_b], dma=("ld", cst_b))
    P.add("dve", lambda e: e.memset(ones32[:], 1.0), writes=[cst_b])
    P.add("dve", lambda e: e.memset(ones16[:], 1.0), writes=[cst_b])
    P.add("pool", lambda e: e.memset(identb[:], 1.0), writes=[cst_b])
    P.add("pool", lambda e: e.affine_select(out=identb[:], in_=identb[:], pattern=[[-1, 128]], compare_op=ALU.is_equal, fill=0.0,
                                            base=0, channel_multiplier=1), reads=[cst_b], writes=[cst_b])
    P.add("dve", lambda e: e.memset(epst[:, 0:1], float(EPS_RES)), writes=[cst_b])
    P.add("dve", lambda e: e.memset(epst[:, 1:2], float(LN_EPS)), writes=[cst_b])
    lr, lrb = rf_next()
    lr2, lr2b = rf_next()
    P.add("sp", lambda e: e.dma_start(out=lr[:], in_=dr["lamrep"][:, 0:512]), writes=[lrb], dma=("ld", lrb))
    P.add("sp", lambda e: e.dma_start(out=lr2[:], in_=dr["lamrep"][:, 512:1024]), writes=[lr2b], dma=("ld", lr2b))
    for l in range(DEPTH):
        src, srcb = (lr, lrb) if l < 2 else (lr2, lr2b)
        o = (l % 2) * 256
        lam_init = 0.8 - 0.6 * math.exp(-0.3 * l)
        for q in range(2):
            a0 = src[:, o + q * 128:o + q * 128 + 64]
            a1 = src[:, o + q * 128 + 64:o + q * 128 + 128]
            tt("dve", a0, a0, a1, ALU.mult, [srcb], [srcb])
            P.add("dve", lambda e, a0=a0, l=l, q=q: e.reduce_sum(out=lamt[:, l, 1 + q:2 + q], in_=a0, axis=mybir.AxisListType.X),
                  reads=[srcb], writes=[lam_b])
        act(lamt[:, l, 1:3], lamt[:, l, 1:3], AF.Exp, [lam_b], [lam_b])
        stt("dve", lamt[:, l, 0:1], lamt[:, l, 2:3], -lam_init, lamt[:, l, 1:2], ALU.add, ALU.subtract, [lam_b], [lam_b])
        ts("dve", subg[:, l, :], pcol("subln", l * 4, 4), 1.0 - lam_init, None, ALU.mult, None, [par_b], [lam_b])

    P.phase = "mod"
    o_c, _ = _PL["cond"]
    condv = par[:, o_c:o_c + 16].rearrange("p (k g) -> p k g", k=8)
    act(sc16[:], condv, AF.Silu, [par_b], [cst_b])
    MOD_BANK = 7

    def mod_closures(l):
        pm_, pmb = psum[MOD_BANK], psum_b[MOD_BANK]
        fns = []

        def tile(jt):
            (wv,), wb = wload([(0, 8, 256, wrows(dr["w_mod"][l], 0, 8, jt * 256, 256))])
            for jj in range(2):
                j = jt * 2 + jj
                for kc in range(8):
                    mm(pm_[:, j * 2:j * 2 + 2], wv[:, kc, jj * 128:(jj + 1) * 128], sc16[:, kc, :], kc == 0, kc == 7,
                       [wb, cst_b], [pmb])

        def fin():
            ob, _ = _PL["b_mod"]
            tt("dve", modT[:, l], pm_[:, 0:144].rearrange("p (j g) -> p j g", g=2),
               apm(par[:, ob + l * 72: ob + l * 72 + 1], [[1, 72], [0, 2]]), ALU.add, [pmb, par_b], [modT_bs[l]])
            for g in range(2):
                for i in range(3):
                    ts("dve", md[:, l, g, i, 0, :], modT[:, l, (3 * i + 1) * 8:(3 * i + 2) * 8, g], 1.0, None, ALU.add, None,
                       [modT_bs[l]], [md_bs[l]])
                    P.add("dve", lambda e, g=g, i=i: e.tensor_copy(out=md[:, l, g, i, 1, :], in_=modT[:, l, (3 * i) * 8:(3 * i + 1) * 8, g]),
                          reads=[modT_bs[l]], writes=[md_bs[l]])
                    ts("dve", md[:, l, g, i, 2, :], modT[:, l, (3 * i + 2) * 8:(3 * i + 3) * 8, g], (0.5 if i != 1 else 1.0) / ALPHA, None,
                       ALU.mult, None, [modT_bs[l]], [md_bs[l]])

        for jt in range(36):
            fns.append(lambda jt=jt: tile(jt))
        fns.append(fin)
        return fns

    inject_mod = (0 in groups)
    for l in range(n_layers if not inject_mod else 1):
        for fn in mod_closures(l):
            fn()

    def modulate(l, g, i, tok0, ntok, dst, dst_bufs_by_half, half0):
        tbs = set(range(tok0 // HB, (tok0 + ntok) // HB))
        dq_flush(lambda tag: tag.get("tb") in tbs)
        for c in range(8):
            for hh in range(ntok // HB):
                tb = (tok0 // HB) + hh
                src = xs[:, c, tok0 + hh * HB: tok0 + (hh + 1) * HB]
                out = dst[:, c, hh * HB:(hh + 1) * HB]
                A = md[:, l, g, i, 0, c:c + 1]
                B = md[:, l, g, i, 1, c:c + 1]
                if (c + hh) % 2 == 0:
                    act(out, src, AF.Identity, [xs_b[c][tb], md_bs[l]], [dst_bufs_by_half[hh][c]], bias=B, scale=A)
                else:
                    ts("dve", out, src, A, B, ALU.mult, ALU.add, [xs_b[c][tb], md_bs[l]], [dst_bufs_by_half[hh][c]])

    DQ = []
    st_state = {"next": 0}

    def dq_push(tag, fn):
        DQ.append((tag, fn))

    def dq_pop(n=1):
        for _ in range(n):
            if not DQ:
                return
            _, fn = DQ.pop(0)
            fn()

    DQ_STATE = {"exposed": False}

    def dq_flush(pred=None):
        last = -1
        for i_, (tag, _) in enumerate(DQ):
            if pred is None or pred(tag):
                last = i_
        DQ_STATE["exposed"] = True
        for _ in range(last + 1):
            _, fn = DQ.pop(0)
            fn()
        DQ_STATE["exposed"] = False

    def ln_stats(chunks, nfeat, eps):
        k = st_state["next"]
        st_state["next"] = 1 - k
        dq_flush(lambda tag: tag.get("sset") == k)
        mean, mean_b, rstd, rstd_b, tmp, tmp_b = stats[2 * k], stats_b[2 * k], stats[2 * k + 1], stats_b[2 * k + 1], stats[4], stats_b[4]
        s1, s1b = ps_next()
        s2, s2b = ps_next()
        n = len(chunks)
        for ci, (ap_, bufs) in enumerate(chunks):
            sq, sqb = rf_next()
            sq16 = sq[:, 0:HB // 2].bitcast(BF16)
            y16 = sq[:, HB // 2:HB].bitcast(BF16)
            if ci < 5:
                act(y16, ap_, AF.Identity, bufs, [sqb])
                act(sq16, ap_, AF.Square, bufs, [sqb])
            else:
                P.add("dve", lambda e, y16=y16, ap_=ap_: e.tensor_copy(out=y16, in_=ap_), reads=bufs, writes=[sqb])
                tt("dve", sq16, ap_, ap_, ALU.mult, bufs, [sqb])
            mm(s1[:], ones16[:], y16, ci == 0, ci == n - 1, [cst_b, sqb], [s1b])
            mm(s2[:], ones16[:], sq16, ci == 0, ci == n - 1, [cst_b, sqb], [s2b])
        inv = 1.0 / nfeat
        act(mean[:], s1[:], AF.Identity, [s1b], [mean_b], scale=inv)
        act(tmp[:], s1[:], AF.Square, [s1b], [tmp_b], scale=inv)
        stt("dve", tmp[:], s2[:], inv, tmp[:], ALU.mult, ALU.subtract, [s2b, tmp_b], [tmp_b])
        act(rstd[:], tmp[:], AF.Ln, [tmp_b, cst_b], [rstd_b], bias=epsc(eps), scale=1.0)
        act(rstd[:], rstd[:], AF.Exp, [rstd_b], [rstd_b], scale=-0.5)
        return k

    def epsc(v):
        return epst[:, 0:1] if v == EPS_RES else epst[:, 1:2]

    def ln_apply_res(l, i, tb, k):
        og, _ = _PL["ln_g"]
        ob, _ = _PL["ln_b"]
        mean, mean_b, rstd, rstd_b = stats[2 * k], stats_b[2 * k], stats[2 * k + 1], stats_b[2 * k + 1]

        def one(c):
            x_ = xs[:, c, tb * HB:(tb + 1) * HB]
            xb_ = xs_b[c][tb]
            exposed = DQ_STATE["exposed"]
            tt(LN_ENG if (not exposed or c % 2 == 0) else "dve", x_, x_, mean[:], ALU.subtract, [xb_, mean_b], [xb_])
            tt("dve", x_, x_, rstd[:], ALU.mult, [xb_, rstd_b], [xb_])
            kk = (l * 3 + i) * 8 + c
            if exposed:
                act(x_, x_, AF.Identity, [xb_, par_b], [xb_], bias=par[:, ob + kk:ob + kk + 1], scale=par[:, og + kk:og + kk + 1])
            else:
                ts("dve", x_, x_, par[:, og + kk:og + kk + 1], par[:, ob + kk:ob + kk + 1], ALU.mult, ALU.add, [xb_, par_b], [xb_])

        for c in range(8):
            dq_push({"tb": tb, "sset": k}, lambda c=c: one(c))

    def ffn(l, f, g, tok0, do_mod=True, post_in_hook=None):
        P.phase = f"ffn"
        i = 0 if f == 0 else 2
        tb0 = tok0 // HB
        if do_mod:
            modulate(l, g, i, tok0, 1024, hT, hT_b, 0)
        w_in2 = dr["w_ffn_in"][l, f]
        w_out2 = dr["w_ffn_out"][l, f]
        for j in range(NJ):
            (wg, wu), wb = wload([(0, 8, 128, wrows(w_in2, 0, 8, j * 128, 128)),
                                  (1024, 8, 128, wrows(w_in2, 0, 8, DFF + j * 128, 128))])
            for hh in range(2):
                G, Gb = ps_next()
                U, Ub = ps_next()
                for kc in range(8):
                    mm(G[:], wg[:, kc, :], hT[:, kc, hh * HB:(hh + 1) * HB], kc == 0, kc == 7, [wb, hT_b[hh][kc]], [Gb])
                for kc in range(8):
                    mm(U[:], wu[:, kc, :], hT[:, kc, hh * HB:(hh + 1) * HB], kc == 0, kc == 7, [wb, hT_b[hh][kc]], [Ub])
                sg, sgb = rf_next()
                act(sg[:], G[:], AF.Silu, [Gb], [sgb])
                tt("dve", aT[:, j, hh * HB:(hh + 1) * HB], sg[:], U[:], ALU.mult, [sgb, Ub], [aT_b[j][hh]])
                dq_pop(1)
        if post_in_hook is not None:
            post_in_hook()
        for m in range(8):
            (w0,), wb0 = wload([(0, 11, 128, wrows(w_out2, 0, 11, m * 128, 128))])
            (w1,), wb1 = wload([(0, 11, 128, wrows(w_out2, 11 * 128, 11, m * 128, 128))])
            for hh in range(2):
                O, Ob = ps_next()
                for j in range(NJ):
                    wv, wb = (w0, wb0) if j < 11 else (w1, wb1)
                    mm(O[:], wv[:, j % 11, :], aT[:, j, hh * HB:(hh + 1) * HB], j == 0, j == NJ - 1, [wb, aT_b[j][hh]], [Ob])
                x_ = xs[:, m, tok0 + hh * HB: tok0 + (hh + 1) * HB]
                stt("dve", x_, O[:], md[:, l, g, i, 2, m:m + 1], x_, ALU.mult, ALU.add, [Ob, md_bs[l], xs_b[m][tb0 + hh]], [xs_b[m][tb0 + hh]])
        P.phase = "ffn_ln"
        for hh in range(2):
            tb = tb0 + hh
            k_ = ln_stats([(xs[:, c, tb * HB:(tb + 1) * HB], [xs_b[c][tb]]) for c in range(8)], D, EPS_RES)
            ln_apply_res(l, i, tb, k_)

    ring_sc = RR([4, 5, 6, 7])
    ring_acc = RR([0, 1, 2, 3])

    def mixer_A(l, g, tb, nseq, S, kt_col0, v_chunk0, cu_cols, pb_cols, do_mod=True):
        tok0 = tb * HB
        P.phase = f"mixA_g{g}"
        if do_mod:
            modulate(l, g, 1, tok0, HB, hTm, [hTm_b], 0)
        win = dr["w_in"][l]
        ob_in, _ = _PL["b_in"]
        for hp in range(2):
            (wk,), wb = wload([(0, 8, 256, wrows(win, 0, 8, 512 + hp * 256, 256))])
            for hh2 in range(2):
                h = hp * 2 + hh2
                Z, Zb = ps_next()
                for kc in range(8):
                    mm(Z[:], wk[:, kc, hh2 * 128:(hh2 + 1) * 128], hTm[:, kc, :], kc == 0, kc == 7, [wb, hTm_b[kc]], [Zb])
                k32, k32b = rf_next()
                bcol = par[:, ob_in + l * 42 + 4 + h: ob_in + l * 42 + 5 + h]
                act(k32[:], Z[:], AF.Identity, [Zb, par_b], [k32b], bias=bcol, scale=1.0)
                kdst = KT[:, h, kt_col0:kt_col0 + HB]
                if g == 0:
                    P.add("sp", lambda e, l=l, h=h, tok0=tok0, k32=k32: e.dma_start(
                        out=dr["okT"][l, h * 128:(h + 1) * 128, tok0:tok0 + HB], in_=k32[:]), reads=[k32b], dma=("st", k32b))
                    P.add("dve", lambda e, kdst=kdst, k32=k32: e.tensor_copy(out=kdst, in_=k32[:]), reads=[k32b], writes=[KT_b])
                else:
                    rope(k32, k32b, kdst, KT_b)
                dq_pop(1)
        (wv0,), wvb0 = wload([(0, 8, 256, wrows(win, 0, 8, 1024, 256))])
        (wv1,), wvb1 = wload([(0, 8, 256, wrows(win, 0, 8, 1280, 256))])
        for t4 in range(4):
            Z, Zb = ps_next()
            for (wv_, wvb_, c0) in ((wv0, wvb0, 0), (wv1, wvb1, 256)):
                for kc in range(8):
                    mm(Z[:, c0:c0 + 256], hTm[:, kc, t4 * 128:(t4 + 1) * 128], wv_[:, kc, :], kc == 0, kc == 7, [wvb_, hTm_b[kc]], [Zb])
            vdst = VT[:, v_chunk0 + t4, :]
            if g == 0:
                v32, v32b = rf_next()
                tt("dve", v32[:], Z[:], bv[:], ALU.add, [Zb, bv_b], [v32b])
                P.add("sp", lambda e, l=l, t0=tok0 + t4 * 128, v32=v32: e.dma_start(out=dr["ov"][l, t0:t0 + 128, :], in_=v32[:]),
                      reads=[v32b], dma=("st", v32b))
                act(vdst, v32[:], AF.Identity, [v32b], [VT_b])
            else:
                tt("dve", vdst, Z[:], bv[:], ALU.add, [Zb, bv_b], [VT_b])
            dq_pop(1)
        (wa,), wab = wload([(0, 8, 256, wrows(win, 0, 8, 1536, 256))])
        (wg_,), wgb = wload([(0, 8, 256, wrows(win, 0, 8, 1792, 256))])
        for c in range(2):
            Za, Zab = ps_next()
            Zg, Zgb = ps_next()
            for kc in range(8):
                mm(Za[:], wa[:, kc, c * 128:(c + 1) * 128], hTm[:, kc, :], kc == 0, kc == 7, [wab, hTm_b[kc]], [Zab])
            for kc in range(8):
                mm(Zg[:], wg_[:, kc, c * 128:(c + 1) * 128], hTm[:, kc, :], kc == 0, kc == 7, [wgb, hTm_b[kc]], [Zgb])
            sg, sgb = rf_next()
            act(sg[:], Zg[:], AF.Sigmoid, [Zgb, par_b], [sgb], bias=par[:, ob_in + l * 42 + 14 + c: ob_in + l * 42 + 15 + c], scale=1.0)
            ba = par[:, ob_in + l * 42 + 12 + c: ob_in + l * 42 + 13 + c]
            for s in range(nseq):
                stt("dve", cu[:, c, cu_cols[s]:cu_cols[s] + S if nseq > 1 else cu_cols[s] + HB],
                    Za[:, s * S:(s + 1) * S] if nseq > 1 else Za[:], ba,
                    sg[:, s * S:(s + 1) * S] if nseq > 1 else sg[:], ALU.add, ALU.mult, [Zab, sgb, par_b], [cu_b])
            dq_pop(1)
        (wp,), wpb_ = wload([(0, 8, 256, wrows(win, 0, 8, 2048, 256))])
        for c in range(2):
            Zp, Zpb = ps_next()
            for kc in range(8):
                mm(Zp[:], wp[:, kc, c * 128:(c + 1) * 128], hTm[:, kc, :], kc == 0, kc == 7, [wpb_, hTm_b[kc]], [Zpb])
            bp = par[:, ob_in + l * 42 + 16 + c: ob_in + l * 42 + 17 + c]
            for s in range(nseq):
                act(pbuf[:, c, pb_cols[s]:pb_cols[s] + (S if nseq > 1 else HB)],
                    Zp[:, s * S:(s + 1) * S] if nseq > 1 else Zp[:], AF.Identity, [Zpb, par_b], [pbuf_b], bias=bp, scale=1.0)

    def rope(q32, q32b, dst, dst_b, split=None):
        Rq, Rqb = ps_next()
        mm(Rq[:], permR[:], q32[:], True, True, [cst_b, q32b], [Rqb])
        t2, t2b = rf_next()
        tt("dve", t2[:], Rq[:], sinh[:], ALU.mult, [Rqb, rope_b], [t2b])
        tt("dve", q32[:], q32[:], cosh[:], ALU.mult, [q32b, rope_b], [q32b])
        if split is None:
            tt("dve", dst, q32[:], t2[:], ALU.add, [q32b, t2b], [dst_b])
        else:
            tt("dve", split[0], q32[0:64, :], t2[0:64, :], ALU.add, [q32b, t2b], dst_b)
            tt("dve", split[1], q32[64:128, :], t2[64:128, :], ALU.add, [q32b, t2b], dst_b)

    def rope_tables(tb):
        for name_r, name_c, dstt in (("crow", "ccol", cosh), ("srow", "scol", sinh)):
            orow, _ = _PL[name_r]
            ocol, _ = _PL[name_c]
            a_r = apm(par[:, orow + tb * 8: orow + tb * 8 + 1], [[1, 8], [0, 64]])
            a_c = apm(par[:, ocol:ocol + 1], [[0, 8], [1, 64]])
            tt("dve", dstt[:].rearrange("p (r c) -> p r c", r=8), a_r, a_c, ALU.mult, [par_b], [rope_b])

    def attention(l, g, nq, qcol0, key_chunks, tail_hook=None):
        nk = len(key_chunks)
        items = [(h, m, ki) for h in range(H) for m in range(2) for ki in range(nk)]
        LA = 3
        DEFER = min(10, nk)
        acc = {}
        tm = {}
        pending = []

        def finish_group(h, m):
            O, Ob, Sm, Smb = acc[(h, m)]
            r_, rb = rf_next()
            if g == 0:
                act(r_[:, 0:nq], Sm[:, 0:nq], AF.Ln, [Smb], [rb])
                act(r_[:, 0:nq], r_[:, 0:nq], AF.Exp, [rb], [rb], scale=-1.0)
            else:
                P.add("dve", lambda e: e.reciprocal(out=r_[:, 0:nq], in_=Sm[:, 0:nq]), reads=[Smb], writes=[rb])
            t_, tb_ = rf_next()
            tt("dve", t_[:, 0:nq], O[:, 0:nq], r_[:, 0:nq], ALU.mult, [Ob, rb], [tb_])
            tm[(h, m)] = (t_, tb_)

        def finish_head_a(h):
            (t0, t0b), (t1, t1b) = tm[(h, 0)], tm[(h, 1)]
            stt("dve", t0[:, 0:nq], t1[:, 0:nq], lamt[:, l, 0:1], t0[:, 0:nq], ALU.mult, ALU.add, [t1b, t0b, lam_b], [t0b])
            if nk < 8:
                sq, sqb = rf_next()
                act(sq[:, 0:HB // 2].bitcast(BF16)[:, 0:nq], t0[:, 0:nq], AF.Square, [t0b], [sqb])
                return (t0, t0b, sq, sqb)
            return (t0, t0b)

        def finish_head_b(h, st_):
            if len(st_) == 4:
                t0, t0b, sq, sqb = st_
            else:
                t0, t0b = st_
                sq, sqb = rf_next()
                act(sq[:, 0:HB // 2].bitcast(BF16)[:, 0:nq], t0[:, 0:nq], AF.Square, [t0b], [sqb])
            Ms, Msb = ps_next(ring_sc)
            mm(Ms[:, 0:nq], ones16[:], sq[:, 0:HB // 2].bitcast(BF16)[:, 0:nq], True, True, [cst_b, sqb], [Msb])
            act(sq[:, 0:nq], Ms[:, 0:nq], AF.Ln, [Msb, cst_b], [sqb], bias=epsc(LN_EPS), scale=1.0 / 128.0)
            act(sq[:, 0:nq], sq[:, 0:nq], AF.Exp, [sqb], [sqb], scale=-0.5)
            stt("dve", aoT[:, h, qcol0:qcol0 + nq], t0[:, 0:nq], subg[:, l, h:h + 1], sq[:, 0:nq], ALU.mult, ALU.mult,
                [t0b, sqb, lam_b], [aoT_b[h]])

        n = len(items)
        for idx in range(n + LA):
            if idx < n:
                h, m, ki = items[idx]
                if ki == 0:
                    O, Ob = ps_next(ring_acc)
                    Sm, Smb = ps_next(ring_acc)
                    acc[(h, m)] = (O, Ob, Sm, Smb)
                kcol, vch = key_chunks[ki]
                p0 = m * 64
                Sc, Scb = ps_next(ring_sc)
                mm(Sc[:, 0:nq], KT[:, h, kcol:kcol + 128], QT[:, h, m, qcol0:qcol0 + nq], True, True,
                   [KT_b, QT_b[h]], [Scb])
                pi = idx % NPT
                act(PT[pi][:, 0:nq], Sc[:, 0:nq], AF.Exp, [Scb], [PT_b[pi]], scale=0.125)
                if idx % 4 == 3:
                    dq_pop(1)
            j = idx - LA
            if j >= 0:
                h, m, ki = items[j]
                kcol, vch = key_chunks[ki]
                O, Ob, Sm, Smb = acc[(h, m)]
                pi = j % NPT
                mm(O[:, 0:nq], VT[:, vch, h * 128:(h + 1) * 128], PT[pi][:, 0:nq], ki == 0, ki == nk - 1, [VT_b, PT_b[pi]], [Ob])
                mm(Sm[:, 0:nq], ones16[:], PT[pi][:, 0:nq], ki == 0, ki == nk - 1, [cst_b, PT_b[pi]], [Smb])
                if ki == nk - 1:
                    finish_group(h, m)
                    if m == 1:
                        st_ = finish_head_a(h)
                        pending.append((j + DEFER, h, st_))
                while pending and pending[0][0] <= j:
                    _, hh_, st_ = pending.pop(0)
                    finish_head_b(hh_, st_)
        if tail_hook is not None:
            tail_hook()
        for (_, hh_, st_) in pending:
            finish_head_b(hh_, st_)

    def mixer_B(l, g, tb, nseq, S, cu_cols, pb_cols, attn_specs, edge_specs, do_mod=True, post_merge_hook=None):
        tok0 = tb * HB
        win = dr["w_in"][l]
        ob_in, _ = _PL["b_in"]
        P.phase = f"mixB_q_g{g}"
        if g == 1 and do_mod:
            modulate(l, g, 1, tok0, HB, hTm, [hTm_b], 0)
            rope_tables(tb)
        def pool_block():
            P.phase = "pool"
            PBS = SEQ + 16
            for c in range(2):
                acc, accb = rf_next()
                for gi in range(2):
                    w = WINS[2 * c + gi]
                    p0 = gi * 64
                    first = True
                    for jx in range(-w // 2 + 1, w // 2):
                        if nseq > 1:
                            a_ = apm(acc[p0:p0 + 64, 0:1], [[S, nseq], [1, S]])
                            src = apm(pbuf[p0:p0 + 64, c, pb_cols[0] + jx: pb_cols[0] + jx + 1], [[PBS, nseq], [1, S]])
                            src0 = apm(pbuf[p0:p0 + 64, c, pb_cols[0] - w // 2: pb_cols[0] - w // 2 + 1], [[PBS, nseq], [1, S]])
                        else:
                            a_ = acc[p0:p0 + 64, 0:HB]
                            src = pbuf[p0:p0 + 64, c, pb_cols[0] + jx: pb_cols[0] + jx + HB]
                            src0 = pbuf[p0:p0 + 64, c, pb_cols[0] - w // 2: pb_cols[0] - w // 2 + HB]
                        if first:
                            tt("dve", a_, src0, src, ALU.add, [pbuf_b], [accb])
                        else:
                            tt("dve", a_, a_, src, ALU.add, [pbuf_b, accb], [accb])
                        first = False
                if nseq > 1:
                    for (col, nm) in ((0, "corrL"), (S - 8, "corrR")):
                        oc, _ = _PL[nm]
                        e_ = apm(acc[:, col:col + 1], [[S, nseq], [1, 8]])
                        tt("dve", e_, e_, apm(par[:, oc + c * 8: oc + c * 8 + 1], [[0, nseq], [1, 8]]), ALU.mult, [accb, par_b], [accb])
                    stt("dve", pmT[:, c, :].rearrange("p (s n) -> p s n", s=nseq), acc[:].rearrange("p (s n) -> p s n", s=nseq),
                        pcol("invw", c), apm(pbuf[:, c, pb_cols[0]: pb_cols[0] + 1], [[PBS, nseq], [1, S]]),
                        ALU.mult, ALU.subtract, [accb, par_b, pbuf_b], [pmT_b])
                else:
                    for (col, which) in edge_specs:
                        oc, _ = _PL["corrL" if which == 0 else "corrR"]
                        tt("dve", acc[:, col:col + 8], acc[:, col:col + 8], par[:, oc + c * 8: oc + c * 8 + 8], ALU.mult, [accb, par_b], [accb])
                    stt("dve", pmT[:, c, :], acc[:], pcol("invw", c), pbuf[:, c, pb_cols[0]: pb_cols[0] + HB],
                        ALU.mult, ALU.subtract, [accb, par_b, pbuf_b], [pmT_b])
                def pg_part(c=c):
                    Pg, Pgb = ps_next(ring_sc)
                    mm(Pg[:], wpb[:, c, :], pmT[:, c, :], True, True, [wpb_b, pmT_b], [Pgb])
                    act(pgT[:, c, :], Pg[:], AF.Identity, [Pgb, par_b], [pgT_b], scale=pcol("pscale", l * 2 + c), bias=0.0)
                for _ in range(3):
                    dq_push({"need": "merge"}, lambda: None)
                dq_push({"need": "merge"}, pg_part)

        def q_block():
            P.phase = f"mixB_q_g{g}"
            for hp in range(2):
                (wq,), wb = wload([(0, 8, 256, wrows(win, 0, 8, hp * 256, 256))])
                for hh2 in range(2):
                    h = hp * 2 + hh2
                    Z, Zb = ps_next()
                    for kc in range(8):
                        mm(Z[:], wq[:, kc, hh2 * 128:(hh2 + 1) * 128], hTm[:, kc, :], kc == 0, kc == 7, [wb, hTm_b[kc]], [Zb])
                    bcol = par[:, ob_in + l * 42 + h: ob_in + l * 42 + h + 1]
                    qw = [QT_b[h], mgT_b[2 * h], mgT_b[2 * h + 1]]
                    P.add("pool", lambda e, h=h: e.memset(QT[64:128, h, 0, :], 0.0), writes=qw)
                    P.add("pool", lambda e, h=h: e.memset(QT[0:64, h, 1, :], 0.0), writes=qw)
                    if g == 0:
                        act(QT[0:64, h, 0, :], Z[0:64, :], AF.Identity, [Zb, par_b], qw, bias=bcol[0:64, :], scale=1.0)
                        act(QT[64:128, h, 1, :], Z[64:128, :], AF.Identity, [Zb, par_b], qw, bias=bcol[64:128, :], scale=1.0)
                    else:
                        q32, q32b = rf_next()
                        act(q32[:], Z[:], AF.Identity, [Zb, par_b], [q32b], bias=bcol, scale=1.0)
                        rope(q32, q32b, None, qw, split=(QT[0:64, h, 0, :], QT[64:128, h, 1, :]))

        ocw, _ = _PL["convw"]

        def build_diags(eng):
            out = []
            for c in range(2):
                dgs = []
                for (k0, nkk) in ((0, 16), (16, 15)):
                    sk = ws_rr.next()
                    slot, sb = wslot[sk], wslot_b[sk]
                    dv_ = slot[:, 0:nkk * 128].rearrange("p (k n) -> p k n", k=nkk)
                    w_ap = apm(par[:, ocw + (l * 2 + c) * KCONV + k0: ocw + (l * 2 + c) * KCONV + k0 + 1], [[1, nkk], [0, 128]])
                    i_ap = apm(identb[:, 0:1], [[0, nkk], [1, 128]])
                    P.add(eng, lambda e, dv_=dv_, w_ap=w_ap, i_ap=i_ap: e.tensor_tensor(out=dv_, in0=i_ap, in1=w_ap, op=ALU.mult),
                          reads=[par_b, cst_b], writes=[sb])
                    dgs.append((dv_, sb, k0, nkk))
                out.append(dgs)
            return out

        accs = []

        def conv_mm(dg_all):
            P.phase = "conv"
            for c in range(2):
                Cp, Cpb = ps_next()
                for (dv_, sb, k0, nkk) in dg_all[c]:
                    for kk in range(nkk):
                        k = k0 + kk
                        if nseq > 1:
                            rhs = apm(cu[:, c, cu_cols[0] + k - 15: cu_cols[0] + k - 14], [[SEQ + 30, nseq], [1, S]])
                        else:
                            rhs = cu[:, c, cu_cols[0] + k - 15: cu_cols[0] + k - 15 + HB]
                        mm(Cp[:], dv_[:, kk, :], rhs, k == 0, k == KCONV - 1, [sb, cu_b], [Cpb])
                acc, accb = rf_next()
                act(acc[:], Cp[:], AF.Identity, [Cpb, par_b], [accb], bias=pcol("convb", l * 2 + c), scale=1.0)
                accs.append((acc, accb))

        if g == 0:
            dg_all = build_diags("dve")
            pool_block()
            conv_mm(dg_all)
            q_block()
        else:
            q_block()
            conv_mm(build_diags("pool"))
        P.phase = "conv"
        kc_ = ln_stats([(a_[:], [ab_]) for (a_, ab_) in accs], 256, LN_EPS)
        for c in range(2):
            acc, accb = accs[c]
            tt("dve", acc[:], acc[:], stats[2 * kc_][:], ALU.subtract, [accb, stats_b[2 * kc_]], [accb])
            tt("dve", acc[:], acc[:], stats[2 * kc_ + 1][:], ALU.mult, [accb, stats_b[2 * kc_ + 1]], [accb])
            act(acc[:], acc[:], AF.Identity, [accb, par_b], [accb], bias=pcol("convlb", l * 2 + c), scale=pcol("convlg", l * 2 + c))
            act(cuo[:, c, :], acc[:], AF.Silu, [accb], [cuo_b])
        if g == 1:
            pool_block()
        P.phase = f"attn_g{g}"
        for (nq, qcol0, key_chunks) in attn_specs:
            attention(l, g, nq, qcol0, key_chunks)
        dq_flush(lambda tag: tag.get("need") == "merge")
        P.phase = "merge"
        for fc in range(8):
            (wg2,), wgab = wload([(0, 8, 256, wrows(win, 0, 8, 2304 + fc * 256, 256))])
            wga, wgc, wgcb = wg2[:, :, 0:128], wg2[:, :, 128:256], wgab
            (wgp,), wgpb = wload([(0, 8, 128, wrows(win, 0, 8, 2304 + 2048 + fc * 128, 128))])
            (wacp,), wob = wload([(0, 8, 128, wrows(dr["w_acp"][l], 0, 8, fc * 128, 128))])
            wao, wco, wpo = wacp[:, 0:4, :], wacp[:, 4:6, :], wacp[:, 6:8, :]
            gts = []
            for gi, (wg_, wgb_) in enumerate(((wga, wgab), (wgc, wgcb), (wgp, wgpb))):
                Zg, Zgb = ps_next()
                for kc in range(8):
                    mm(Zg[:], wg_[:, kc, :], hTm[:, kc, :], kc == 0, kc == 7, [wgb_, hTm_b[kc]], [Zgb])
                sg, sgb = rf_next()
                zc = 18 + gi * 8 + fc
                act(sg[:], Zg[:], AF.Sigmoid, [Zgb, par_b], [sgb], bias=par[:, ob_in + l * 42 + zc: ob_in + l * 42 + zc + 1], scale=1.0)
                gts.append((sg, sgb))
            A_, Ab = ps_next()
            for h in range(H):
                mm(A_[:], wao[:, h, :], aoT[:, h, :], h == 0, h == H - 1, [wob, aoT_b[h]], [Ab])
            C_, Cb = ps_next()
            for c in range(2):
                mm(C_[:], wco[:, c, :], cuo[:, c, :], c == 0, c == 1, [wob, cuo_b], [Cb])
            Pp, Ppb = ps_next()
            for c in range(2):
                mm(Pp[:], wpo[:, c, :], pgT[:, c, :], c == 0, c == 1, [wob, pgT_b], [Ppb])
            (ga, gab), (gc, gcb), (gp, gpb) = gts
            tt("dve", ga[:], ga[:], A_[:], ALU.mult, [gab, Ab], [gab])
            tt("dve", gc[:], gc[:], C_[:], ALU.mult, [gcb, Cb], [gcb])
            tt("dve", gp[:], gp[:], Pp[:], ALU.mult, [gpb, Ppb], [gpb])
            tt("dve", ga[:], ga[:], gc[:], ALU.add, [gab, gcb], [gab])
            tt("dve", mgT[:, fc, :], ga[:], gp[:], ALU.add, [gab, gpb], [mgT_b[fc], QT_b[fc // 2]])
            dq_pop(1)
        if post_merge_hook is not None:
            post_merge_hook()
        P.phase = "outproj"
        obo, _ = _PL["b_out"]
        for fp in range(4):
            (wo,), wob = wload([(0, 8, 256, wrows(dr["w_out"][l], 0, 8, fp * 256, 256))])
            for f2 in range(2):
                fo = fp * 2 + f2
                O, Ob = ps_next()
                for fc in range(8):
                    mm(O[:], wo[:, fc, f2 * 128:(f2 + 1) * 128], mgT[:, fc, :], fc == 0, fc == 7, [wob, mgT_b[fc]], [Ob])
                t_, tb_ = rf_next()
                act(t_[:], O[:], AF.Identity, [Ob, par_b], [tb_], bias=par[:, obo + l * 8 + fo: obo + l * 8 + fo + 1], scale=1.0)
                x_ = xs[:, fo, tok0:tok0 + HB]
                stt("dve", x_, t_[:], md[:, l, g, 1, 2, fo:fo + 1], x_, ALU.mult, ALU.add, [tb_, md_bs[l], xs_b[fo][tb]], [xs_b[fo][tb]])
        P.phase = "mix_ln"
        k_ = ln_stats([(xs[:, c, tok0:tok0 + HB], [xs_b[c][tb]]) for c in range(8)], D, EPS_RES)
        ln_apply_res(l, 1, tb, k_)

    def mixer(l, g):
        phase_sync(True)
        P.add("sp", lambda e: e.dma_start(out=bv[:], in_=dr["bvrep"][l]), writes=[bv_b], dma=("ld", bv_b))
        P.add("pool", lambda e: e.dma_start(out=wpb[:], in_=dr["wpbd"][l].rearrange("c p n -> p c n")), writes=[wpb_b], dma=("ld", wpb_b))
        if g == 0:
            for tb in range(TP // HB):
                cu_cols = [15 + s * (SEQ + 30) for s in range(2)]
                pb_cols = [8 + s * (SEQ + 16) for s in range(2)]
                mixer_A(l, g, tb, 2, SEQ, tb * HB, tb * 4, cu_cols, pb_cols, do_mod=(tb == 0))
                specs = [(SEQ, s * SEQ, [(tb * HB + s * SEQ + kk * 128, tb * 4 + s * 2 + kk) for kk in range(2)]) for s in range(2)]
                edges = [(s * SEQ, 0) for s in range(2)] + [(s * SEQ + SEQ - 8, 1) for s in range(2)]
                hook = (lambda tb=tb: modulate(l, g, 1, (tb + 1) * HB, HB, hTm, [hTm_b], 0)) if tb + 1 < TP // HB else None
                mixer_B(l, g, tb, 2, SEQ, cu_cols, pb_cols, specs, edges, post_merge_hook=hook)
        else:
            P.add("pool", lambda e: e.dma_start(out=KT[:, :, 0:PAST], in_=dr["ckT"][l].rearrange("(h p) t -> p h t", p=128)),
                  writes=[KT_b], dma=("ld", KT_b))
            P.add("pool", lambda e: e.dma_start(out=VT[:, 0:4, :], in_=dr["cv"][l].rearrange("(k p) n -> p k n", p=128)),
                  writes=[VT_b], dma=("ld", VT_b))
            for tb in range(TS // HB):
                rope_tables(tb)
                mixer_A(l, g, tb, 1, TS, PAST + tb * HB, 4 + tb * 4, [15 + tb * HB], [8 + tb * HB])
            keys = [(kk * 128, kk) for kk in range(20)]
            for tb in range(TS // HB):
                edges = ([(0, 0)] if tb == 0 else []) + ([(HB - 8, 1)] if tb == TS // HB - 1 else [])
                def hook_s(tb=tb):
                    modulate(l, g, 1, (tb + 1) * HB, HB, hTm, [hTm_b], 0)
                    rope_tables(tb + 1)
                mixer_B(l, g, tb, 1, TS, [15 + tb * HB], [8 + tb * HB], [(HB, 0, keys)], edges, do_mod=(tb == 0),
                        post_merge_hook=hook_s if tb + 1 < TS // HB else None)
        phase_sync(False)

    def run_ffn(l, f, g, T):
        nmt = T // 1024
        for mt in range(nmt):
            hook = None
            if mt + 1 < nmt:
                hook = (lambda mt=mt: modulate(l, g, 0 if f == 0 else 2, (mt + 1) * 1024, 1024, hT, hT_b, 0))
            ffn(l, f, g, mt * 1024, do_mod=(mt == 0), post_in_hook=hook)

    for g in groups:
        T = TP if g == 0 else TS
        xin = dr["xpT"] if g == 0 else dr["xsT"]
        yout = dr["ypT"] if g == 0 else dr["ysT"]
        for c in range(8):
            P.add("sp", lambda e, c=c, xin=xin, T=T: e.dma_start(out=xs[:, c, 0:T], in_=xin[c * 128:(c + 1) * 128, :]),
                  writes=[xs_b[c][tb] for tb in range(T // HB)], dma=("ld", xs_b[c][0]))
        P.add("dve", lambda e: e.memset(cu[:], 0.0), writes=[cu_b])
        P.add("dve", lambda e: e.memset(pbuf[:], 0.0), writes=[pbuf_b])
        phase_sync(False)
        for l in range(n_layers):
            if g == 0 and inject_mod and l + 1 < n_layers:
                ring_all.idx = [k_ for k_ in range(8) if k_ != MOD_BANK]
                for fn in mod_closures(l + 1):
                    dq_push({"mod": l + 1}, fn)
            run_ffn(l, 0, g, T)
            if g == 0 and inject_mod and l + 1 < n_layers:
                dq_flush(lambda tag: tag.get("mod") == l + 1)
                ring_all.idx = list(range(8))
            mixer(l, g)
            run_ffn(l, 1, g, T)
        dq_flush()
        for c in range(8):
            P.add("sp", lambda e, c=c, yout=yout, T=T: e.dma_start(out=yout[c * 128:(c + 1) * 128, :], in_=xs[:, c, 0:T]),
                  reads=[xs_b[c][tb] for tb in range(T // HB)], dma=("st", xs_b[c][0]))
    _LAYOUT_INFO["pe_phases"] = P.pe_phases
    P.emit_all()
    return nc


def _fm(v):
    v = np.asarray(v, np.float32)
    lead = v.shape[:-1]
    n = v.shape[-1] // 128
    return np.moveaxis(v.reshape(lead + (n, 128)), -1, 0).reshape(128, -1)


def _const_tables():
    par = np.zeros((128, NPAR), np.float32)
    invw = np.zeros((128, 2), np.float32)
    corrL = np.ones((128, 2, 8), np.float32)
    corrR = np.ones((128, 2, 8), np.float32)
    for c in range(2):
        for gi in range(2):
            w = WINS[2 * c + gi]
            sl = slice(gi * 64, gi * 64 + 64)
            invw[sl, c] = 1.0 / w
            for e in range(8):
                cntL = min(e + w // 2, 10 ** 9) - max(e - w // 2, 0)
                corrL[sl, c, e] = w / cntL
                hi_off = min(-8 + e + w // 2, 0)
                lo_off = -8 + e - w // 2
                corrR[sl, c, e] = w / (hi_off - lo_off)
    inv = 10000.0 ** (-np.arange(16, dtype=np.float32) / 16.0)
    crow = np.ones((128, 32), np.float32); srow = np.ones((128, 32), np.float32)
    ccol = np.ones((128, 64), np.float32); scol = np.ones((128, 64), np.float32)
    for p in range(128):
        dd = p % 64
        seg = dd // 32
        i = (dd % 32) % 16
        if seg == 0:
            ang = np.arange(32, dtype=np.float32) * inv[i]
            crow[p] = np.cos(ang); srow[p] = np.sin(ang)
        else:
            ang = np.arange(64, dtype=np.float32) * inv[i]
            ccol[p] = np.cos(ang); scol[p] = np.sin(ang)
    R = np.zeros((128, 128), np.float32)
    for po in range(128):
        if po % 32 < 16:
            R[po + 16, po] = -1.0
        else:
            R[po - 16, po] = 1.0

    def put(name, arr):
        o, w = _PL[name]
        arr = np.asarray(arr, np.float32).reshape(128, -1)
        assert arr.shape[1] == w, (name, arr.shape, w)
        par[:, o:o + w] = arr

    put("invw", invw); put("corrL", corrL); put("corrR", corrR)
    put("crow", crow); put("ccol", ccol); put("srow", srow); put("scol", scol)
    return par, R, put


_NC_CACHE = {}
_LAYOUT_INFO = {}


def _perm_w_in(w_in):
    w = w_in.copy()
    ga = w_in[..., 2304:2304 + 1024].reshape(w_in.shape[:-1] + (8, 1, 128))
    gc = w_in[..., 2304 + 1024:2304 + 2048].reshape(w_in.shape[:-1] + (8, 1, 128))
    w[..., 2304:2304 + 2048] = np.concatenate([ga, gc], axis=-2).reshape(w_in.shape[:-1] + (2048,))
    return w


def prepare_inputs(x_prompt, x_sample, cache_k, cache_v, c, c_ctx, w_mod, b_mod, w_ffn_in, w_ffn_out, ln_g, ln_b, w_in, b_in,
                   lambda_qk, subln_g, w_att_o, conv_dw_w, conv_dw_b, conv_ln_g, conv_ln_b, w_conv_o, w_pool_g, pool_scale,
                   w_pool_o, w_out, b_out, cores=range(NCORE)):
    f32 = lambda a: np.ascontiguousarray(np.asarray(a, dtype=np.float32))
    par0, R, put = _const_tables()
    put("b_mod", _fm(f32(b_mod)))
    put("ln_g", _fm(f32(ln_g))); put("ln_b", _fm(f32(ln_b)))
    put("b_in", _fm(f32(b_in))); put("b_out", _fm(f32(b_out)))
    put("subln", np.moveaxis(f32(subln_g), -1, 0))
    put("convw", np.transpose(f32(conv_dw_w).reshape(DEPTH, KCONV, 2, 128), (3, 0, 2, 1)))
    put("convb", _fm(f32(conv_dw_b))); put("convlg", _fm(f32(conv_ln_g))); put("convlb", _fm(f32(conv_ln_b)))
    put("pscale", _fm(f32(pool_scale)))
    lamrep = np.ascontiguousarray(np.broadcast_to(f32(lambda_qk).reshape(1, -1), (128, DEPTH * 256)))
    wg = f32(w_pool_g)
    wpbd = np.zeros((DEPTH, 2, 128, 128), np.float32)
    for cc in range(2):
        for gi in range(2):
            wpbd[:, cc, gi * 64:(gi + 1) * 64, gi * 64:(gi + 1) * 64] = wg[:, 2 * cc + gi]
    bvrep = np.ascontiguousarray(np.broadcast_to(f32(b_in)[:, None, 1024:1536], (DEPTH, 128, 512)))
    shared = dict(lamrep=lamrep, permR=R, wpbd=wpbd, bvrep=bvrep, w_mod=f32(w_mod), w_ffn_in=f32(w_ffn_in),
                  w_ffn_out=f32(w_ffn_out), w_in=_perm_w_in(f32(w_in)),
                  w_acp=np.ascontiguousarray(np.concatenate([f32(w_att_o), f32(w_conv_o), f32(w_pool_o)], axis=1)),
                  w_out=f32(w_out))
    xp = f32(x_prompt); xsm = f32(x_sample); ck = f32(cache_k); cvv = f32(cache_v); cc_ = f32(c); cctx = f32(c_ctx)
    in_maps = []
    for core in cores:
        par = par0.copy()
        cond = np.stack([cctx, cc_[core]], axis=-1)
        o, w = _PL["cond"]
        par[:, o:o + w] = np.transpose(cond.reshape(8, 128, 2), (1, 0, 2)).reshape(128, 16)
        m = dict(shared)
        m["params"] = par
        m["xpT"] = np.ascontiguousarray(xp[core * PB:(core + 1) * PB].reshape(TP, D).T)
        m["xsT"] = np.ascontiguousarray(xsm[core].T)
        m["ckT"] = np.ascontiguousarray(np.transpose(ck[core].reshape(DEPTH, PAST, 512), (0, 2, 1)))
        m["cv"] = np.ascontiguousarray(cvv[core].reshape(DEPTH, PAST, 512))
        in_maps.append(m)
    return in_maps


def assemble_outputs(results, cores=range(NCORE)):
    cores = list(cores)
    n = len(cores)
    y_prompt = np.empty((n * PB, SEQ, D), np.float32)
    y_sample = np.empty((n, TS, D), np.float32)
    nk = np.empty((n * PB, DEPTH, SEQ, H, 2, 64), np.float32)
    nv = np.empty((n * PB, DEPTH, SEQ, H, 128), np.float32)
    for i in range(n):
        r = results[i]
        y_prompt[i * PB:(i + 1) * PB] = r["ypT"].T.reshape(PB, SEQ, D)
        y_sample[i] = r["ysT"].T
        okT = r["okT"]
        nk[i * PB:(i + 1) * PB] = np.transpose(okT, (2, 0, 1)).reshape(PB, SEQ, DEPTH, H, 2, 64).transpose(0, 2, 1, 3, 4, 5)
        ov = r["ov"]
        nv[i * PB:(i + 1) * PB] = np.transpose(ov.reshape(DEPTH, PB, SEQ, H, 128), (1, 0, 2, 3, 4))
    return (y_prompt, y_sample, nk, nv)


def kernel(**inputs):
    if "nc" not in _NC_CACHE:
        _NC_CACHE["nc"] = build_program()
    nc = _NC_CACHE["nc"]
    in_maps = prepare_inputs(**inputs)
    res = run_bass_kernel_spmd(nc, in_maps, core_ids=list(range(NCORE)))
    return assemble_outputs(res.results)
```
